# Optimizing a Trainium2 kernel written in Bass

```python
import math
import jax, jax.numpy as jnp
from jax import lax
import numpy as np

D_MODEL = 1024
BATCH = 4
SEQ = 4096
DEPTH = 2

PLE_DIM = 256
BLOCK = 128
EPS = 1e-6
ROPE_THETA = 10000.0

MLA_HEADS = 8
MLA_Q_LORA = 384
MLA_KV_LORA = 256
MLA_NOPE = 64
MLA_ROPE = 32
MLA_V = 64
MLA_IN = MLA_Q_LORA + MLA_KV_LORA + MLA_ROPE
MLA_OUT = MLA_HEADS * MLA_V

DIFF_HEADS = 4
DIFF_QK = 64
DIFF_V = 2 * DIFF_QK
DIFF_QK_W = DIFF_HEADS * 2 * DIFF_QK
DIFF_IN = 2 * DIFF_QK_W + DIFF_HEADS * DIFF_V
DIFF_OUT = DIFF_HEADS * DIFF_V

SB_HEADS = 8
SB_DIM = 64
SB_W = SB_HEADS * SB_DIM
SB_IN = 3 * SB_W
SB_OUT = SB_W

D_IN = MLA_IN + DIFF_IN + SB_IN
N_BRANCH = 3
D_FF = 2816

kernel_name = "hybrid_mla_diff_stickbreak_macaron_ple"


def rms_norm(x, g):
    xf = x.astype(jnp.float32)
    y = xf * lax.rsqrt(jnp.mean(xf * xf, axis=-1, keepdims=True) + EPS)
    return (y * g.astype(jnp.float32)).astype(x.dtype)


def swiglu(x, w1, w3, w2):
    return (jax.nn.silu(x @ w1) * (x @ w3)) @ w2


def rope(x, pos):
    half = x.shape[-1] // 2
    freqs = 1.0 / (ROPE_THETA ** (jnp.arange(half, dtype=jnp.float32) / half))
    ang = pos.astype(jnp.float32)[:, None] * freqs[None, :]
    cos = jnp.cos(ang)[:, None, :]
    sin = jnp.sin(ang)[:, None, :]
    xf = x.astype(jnp.float32)
    x1, x2 = xf[..., :half], xf[..., half:]
    return jnp.concatenate([x1 * cos - x2 * sin, x2 * cos + x1 * sin], axis=-1).astype(x.dtype)


def to_blocks(t):
    b, s = t.shape[:2]
    t = t.reshape((b, s // BLOCK, BLOCK) + t.shape[2:])
    return jnp.moveaxis(t, 1, 0)


def from_blocks(t):
    t = jnp.moveaxis(t, 0, 1)
    return t.reshape((t.shape[0], t.shape[1] * t.shape[2]) + t.shape[3:])


def mla_branch(z, g_cq, g_ckv, w_uq, w_ukv, pos, blk_ids):
    b, s, _ = z.shape
    c_q = z[..., :MLA_Q_LORA]
    c_kv = z[..., MLA_Q_LORA:MLA_Q_LORA + MLA_KV_LORA]
    k_rope = z[..., MLA_Q_LORA + MLA_KV_LORA:]
    q = (rms_norm(c_q, g_cq) @ w_uq).reshape(b, s, MLA_HEADS, MLA_NOPE + MLA_ROPE)
    q_nope = q[..., :MLA_NOPE]
    q_rope = rope(q[..., MLA_NOPE:], pos)
    kv = (rms_norm(c_kv, g_ckv) @ w_ukv).reshape(b, s, MLA_HEADS, MLA_NOPE + MLA_V)
    k_nope, v = kv[..., :MLA_NOPE], kv[..., MLA_NOPE:]
    k_rope = rope(k_rope[:, :, None, :], pos)[:, :, 0, :]
    scale = (MLA_NOPE + MLA_ROPE) ** -0.5

    def block(args):
        qn, qr, blk = args
        qpos = blk * BLOCK + jnp.arange(BLOCK)
        sc = (jnp.einsum('bqhd,bkhd->bhqk', qn, k_nope)
              + jnp.einsum('bqhr,bkr->bhqk', qr, k_rope)).astype(jnp.float32) * scale
        sc = jnp.where(pos[None, :] <= qpos[:, None], sc, -jnp.inf)
        pr = jax.nn.softmax(sc, axis=-1).astype(v.dtype)
        return jnp.einsum('bhqk,bkhd->bqhd', pr, v)

    o = from_blocks(lax.map(block, (to_blocks(q_nope), to_blocks(q_rope), blk_ids)))
    return o.reshape(b, s, MLA_OUT)


def diff_branch(z, lq1, lk1, lq2, lk2, g_subln, lam_init, pos, blk_ids):
    b, s, _ = z.shape
    q = z[..., :DIFF_QK_W].reshape(b, s, DIFF_HEADS, 2, DIFF_QK)
    k = z[..., DIFF_QK_W:2 * DIFF_QK_W].reshape(b, s, DIFF_HEADS, 2, DIFF_QK)
    v = z[..., 2 * DIFF_QK_W:].reshape(b, s, DIFF_HEADS, DIFF_V)
    f32 = jnp.float32
    lam = (jnp.exp(jnp.sum(lq1.astype(f32) * lk1.astype(f32)))
           - jnp.exp(jnp.sum(lq2.astype(f32) * lk2.astype(f32))) + lam_init)
    head_idx = jnp.arange(1, DIFF_HEADS + 1, dtype=f32)
    slopes = jnp.exp2(-8.0 * head_idx / DIFF_HEADS)
    scale = DIFF_QK ** -0.5

    def block(args):
        qb, blk = args
        qpos = blk * BLOCK + jnp.arange(BLOCK)
        dist = (qpos[:, None] - pos[None, :]).astype(f32)
        bias = -slopes[:, None, None] * dist
        sc = jnp.einsum('bqhcd,bkhcd->bchqk', qb, k).astype(f32) * scale + bias
        sc = jnp.where(dist >= 0, sc, -jnp.inf)
        pr = jax.nn.softmax(sc, axis=-1)
        a = (pr[:, 0] - lam * pr[:, 1]).astype(v.dtype)
        return jnp.einsum('bhqk,bkhd->bqhd', a, v)

    o = from_blocks(lax.map(block, (to_blocks(q), blk_ids)))
    o = rms_norm(o, g_subln) * (1.0 - lam_init)
    return o.reshape(b, s, DIFF_OUT)


def stick_breaking_branch(z, pos, blk_ids):
    b, s, _ = z.shape
    q = z[..., :SB_W].reshape(b, s, SB_HEADS, SB_DIM)
    k = z[..., SB_W:2 * SB_W].reshape(b, s, SB_HEADS, SB_DIM)
    v = z[..., 2 * SB_W:].reshape(b, s, SB_HEADS, SB_DIM)
    scale = SB_DIM ** -0.5

    def block(args):
        qb, blk = args
        qpos = blk * BLOCK + jnp.arange(BLOCK)
        logits = jnp.einsum('bqhd,bkhd->bhqk', qb, k).astype(jnp.float32) * scale
        strict = pos[None, :] < qpos[:, None]
        log_beta = jax.nn.log_sigmoid(logits)
        log_1m = jnp.where(strict, jax.nn.log_sigmoid(-logits), 0.0)
        suffix = lax.cumsum(log_1m, axis=3, reverse=True) - log_1m
        w = jnp.where(strict, jnp.exp(log_beta + suffix), 0.0).astype(v.dtype)
        return jnp.einsum('bhqk,bkhd->bqhd', w, v)

    o = from_blocks(lax.map(block, (to_blocks(q), blk_ids)))
    return o.reshape(b, s, SB_OUT)


def hybrid_mixer(u, w_in, g_cq, g_ckv, w_uq, w_ukv, lq1, lk1, lq2, lk2, g_subln,
                 w_o_mla, w_o_diff, w_o_sb, w_branch_gate, w_out, lam_init):
    b, s, _ = u.shape
    pos = jnp.arange(s)
    blk_ids = jnp.arange(s // BLOCK)
    z = u @ w_in
    z_mla = z[..., :MLA_IN]
    z_diff = z[..., MLA_IN:MLA_IN + DIFF_IN]
    z_sb = z[..., MLA_IN + DIFF_IN:]
    y_mla = mla_branch(z_mla, g_cq, g_ckv, w_uq, w_ukv, pos, blk_ids)
    y_diff = diff_branch(z_diff, lq1, lk1, lq2, lk2, g_subln, lam_init, pos, blk_ids)
    y_sb = stick_breaking_branch(z_sb, pos, blk_ids)
    gates = jax.nn.sigmoid(u @ w_branch_gate).reshape(b, s, N_BRANCH, D_MODEL)
    merged = (gates[:, :, 0] * (y_mla @ w_o_mla)
              + gates[:, :, 1] * (y_diff @ w_o_diff)
              + gates[:, :, 2] * (y_sb @ w_o_sb))
    return merged @ w_out


def setup_inputs(seed: int = 0) -> dict:
    key = jax.random.key(seed)
    counter = [0]

    def nk():
        counter[0] += 1
        return jax.random.fold_in(key, counter[0])

    def dense(shape, fan_in, gain=1.0):
        return jax.random.normal(nk(), shape, jnp.float32) * (gain * fan_in ** -0.5)

    def norm_gain(shape):
        return 1.0 + 0.02 * jax.random.normal(nk(), shape, jnp.float32)

    def small(shape, std):
        return std * jax.random.normal(nk(), shape, jnp.float32)

    L = DEPTH
    return {
        "x": jax.random.normal(nk(), (BATCH, SEQ, D_MODEL), jnp.float32),
        "p": jax.random.normal(nk(), (DEPTH, BATCH, SEQ, PLE_DIM), jnp.float32),
        "g_ffn1": norm_gain((L, D_MODEL)),
        "w1_a": dense((L, D_MODEL, D_FF), D_MODEL),
        "w3_a": dense((L, D_MODEL, D_FF), D_MODEL),
        "w2_a": dense((L, D_FF, D_MODEL), D_FF),
        "g_mix": norm_gain((L, D_MODEL)),
        "w_in": dense((L, D_MODEL, D_IN), D_MODEL),
        "g_cq": norm_gain((L, MLA_Q_LORA)),
        "g_ckv": norm_gain((L, MLA_KV_LORA)),
        "w_uq": dense((L, MLA_Q_LORA, MLA_HEADS * (MLA_NOPE + MLA_ROPE)), MLA_Q_LORA),
        "w_ukv": dense((L, MLA_KV_LORA, MLA_HEADS * (MLA_NOPE + MLA_V)), MLA_KV_LORA),
        "lambda_q1": small((L, DIFF_QK), 0.1),
        "lambda_k1": small((L, DIFF_QK), 0.1),
        "lambda_q2": small((L, DIFF_QK), 0.1),
        "lambda_k2": small((L, DIFF_QK), 0.1),
        "g_subln": norm_gain((L, DIFF_V)),
        "w_o_mla": dense((L, MLA_OUT, D_MODEL), MLA_OUT),
        "w_o_diff": dense((L, DIFF_OUT, D_MODEL), DIFF_OUT),
        "w_o_sb": dense((L, SB_OUT, D_MODEL), SB_OUT),
        "w_branch_gate": dense((L, D_MODEL, N_BRANCH * D_MODEL), D_MODEL),
        "w_out": dense((L, D_MODEL, D_MODEL), D_MODEL),
        "g_ffn2": norm_gain((L, D_MODEL)),
        "w1_b": dense((L, D_MODEL, D_FF), D_MODEL),
        "w3_b": dense((L, D_MODEL, D_FF), D_MODEL),
        "w2_b": dense((L, D_FF, D_MODEL), D_FF),
        "g_ple": norm_gain((L, D_MODEL)),
        "w_ple_gate": dense((L, D_MODEL, D_MODEL), D_MODEL),
        "w_ple_proj": dense((L, PLE_DIM, D_MODEL), PLE_DIM),
        "g_final": norm_gain((D_MODEL,)),
    }


def reference(x, p, g_ffn1, w1_a, w3_a, w2_a, g_mix, w_in, g_cq, g_ckv, w_uq, w_ukv,
              lambda_q1, lambda_k1, lambda_q2, lambda_k2, g_subln, w_o_mla, w_o_diff, w_o_sb,
              w_branch_gate, w_out, g_ffn2, w1_b, w3_b, w2_b, g_ple, w_ple_gate, w_ple_proj,
              g_final):
    h = x
    for i in range(DEPTH):
        lam_init = 0.8 - 0.6 * math.exp(-0.3 * i)
        h = h + 0.5 * swiglu(rms_norm(h, g_ffn1[i]), w1_a[i], w3_a[i], w2_a[i])
        u = rms_norm(h, g_mix[i])
        h = h + hybrid_mixer(u, w_in[i], g_cq[i], g_ckv[i], w_uq[i], w_ukv[i],
                             lambda_q1[i], lambda_k1[i], lambda_q2[i], lambda_k2[i], g_subln[i],
                             w_o_mla[i], w_o_diff[i], w_o_sb[i], w_branch_gate[i], w_out[i],
                             lam_init)
        h = h + 0.5 * swiglu(rms_norm(h, g_ffn2[i]), w1_b[i], w3_b[i], w2_b[i])
        ple_gate = jax.nn.sigmoid(rms_norm(h, g_ple[i]) @ w_ple_gate[i])
        h = h + ple_gate * (p[i] @ w_ple_proj[i])
    return rms_norm(h, g_final)
```

```python
import math
from contextlib import ExitStack

import ml_dtypes
import numpy as np

import concourse.bass as bass
import concourse.mybir as mybir
from concourse.bass_utils import run_bass_kernel_spmd

F32 = mybir.dt.float32
BF16 = mybir.dt.bfloat16
AF = mybir.ActivationFunctionType
ALU = mybir.AluOpType
AX = mybir.AxisListType

D = 1024
NT = 2048
DFF = 2816
NL = 2
EPS = 1e-6
KT_ROWS = 1568
GL = 38
NGV = 2 * GL + 8
C_DIFF = 672
C_SB = 2208


class T:
    def __init__(self, h, name):
        self.h = h
        self.name = name
        self.lw = None
        self.rd = []
        self.sem = None

    def __getitem__(self, idx):
        return self.h[idx]


class Scope:
    def __init__(self, fw):
        self.fw = fw
        self.es = ExitStack()
        self.tiles = []

    def __enter__(self):
        self.es.__enter__()
        return self

    def __exit__(self, *a):
        self.fw.barrier()
        for t in self.tiles:
            if t.sem is not None:
                self.fw.free_sems[t.sem["kind"]].append(t.sem)
                t.sem = None
        return self.es.__exit__(*a)

    def sbuf(self, name, shape, dtype):
        self.fw.uid += 1
        name = "%s_%d" % (name, self.fw.uid)
        h = self.es.enter_context(self.fw.nc.sbuf_tensor(name, list(shape), dtype))
        t = T(h, name)
        self.tiles.append(t)
        return t

    def psum(self, name, shape, dtype):
        h = self.es.enter_context(self.fw.nc.psum_tensor(name, list(shape), dtype))
        t = T(h, name)
        self.tiles.append(t)
        return t

    def track(self, h, name):
        t = T(h, name)
        self.tiles.append(t)
        return t


class FW:
    def __init__(self, nc, es, n_dma_sems=56):
        self.nc = nc
        self.eng = {}
        for name, e in [("pe", nc.tensor), ("act", nc.scalar), ("dve", nc.vector),
                        ("pool", nc.gpsimd), ("sp", nc.sync)]:
            sem = es.enter_context(nc.semaphore("s_" + name))
            self.eng[name] = dict(e=e, sem=sem, cnt=0, known={}, name=name)
        self.pe_pending = []
        self.free_sems = {"sw": [], "hw": []}
        self.all_dma_sems = []
        for i in range(n_dma_sems):
            s = es.enter_context(nc.semaphore("d%d" % i))
            kind = "sw" if i % 3 == 0 else "hw"
            st = dict(sem=s, cnt=0, kind=kind)
            self.free_sems[kind].append(st)
            self.all_dma_sems.append(st)
        self.n_wait = 0
        self.n_ins = 0
        self.uid = 0

    def scope(self):
        return Scope(self)

    def _wait(self, E, tok):
        if tok is None:
            return
        sem, val, kind = tok
        if kind == "pe" and E["name"] == "pe":
            return
        key = id(sem)
        if E["known"].get(key, 0) >= val:
            return
        E["e"].wait_ge(sem, val)
        E["known"][key] = val
        self.n_wait += 1

    def _deps(self, E, reads, writes, dma_out=None):
        if E["name"] != "pe" and self.pe_pending:
            for t in writes:
                for r, w in self.pe_pending:
                    assert all(t is not x for x in r) and all(t is not x for x in w), "pending PE access to " + t.name
            for t in reads:
                for r, w in self.pe_pending:
                    assert all(t is not x for x in w), "pending PE write to " + t.name
        for t in reads:
            self._wait(E, t.lw)
        for t in writes:
            own = (dma_out is not None and t is dma_out and t.lw is not None and t.lw[2] == "dma"
                   and t.sem is not None and t.lw[0] is t.sem["sem"])
            if not own:
                self._wait(E, t.lw)
            for tok in t.rd:
                self._wait(E, tok)

    def _post(self, tok, reads, writes):
        for t in reads:
            t.rd.append(tok)
            if len(t.rd) > 16:
                d = {}
                for k in t.rd:
                    kk = id(k[0])
                    if kk not in d or d[kk][1] < k[1]:
                        d[kk] = k
                t.rd = list(d.values())
        for t in writes:
            t.lw = tok
            t.rd = []

    def op(self, engname, fn, reads=(), writes=(), inc=True):
        E = self.eng[engname]
        self._deps(E, reads, writes)
        ins = fn(E["e"])
        self.n_ins += 1
        if engname == "pe" and not inc:
            self.pe_pending.append((list(reads), list(writes)))
            return ins
        E["cnt"] += 1
        ins.then_inc(E["sem"], 1)
        tok = (E["sem"], E["cnt"], engname)
        self._post(tok, reads, writes)
        if engname == "pe":
            for r, w in self.pe_pending:
                self._post(tok, r, w)
            self.pe_pending = []
        return ins

    def dma(self, qname, out_t, in_t, out_ap, in_ap, **kw):
        E = self.eng[qname]
        self._deps(E, [in_t], [out_t], dma_out=out_t)
        kind = "sw" if qname == "pool" else "hw"
        if out_t.sem is None:
            out_t.sem = self.free_sems[kind].pop()
        assert out_t.sem["kind"] == kind, "tile %s written by both SW and HW DGE" % out_t.name
        ins = E["e"].dma_start(out=out_ap, in_=in_ap, **kw)
        self.n_ins += 1
        out_t.sem["cnt"] += 16
        ins.then_inc(out_t.sem["sem"], 16)
        tok = (out_t.sem["sem"], out_t.sem["cnt"], "dma")
        self._post(tok, [in_t], [out_t])
        return ins

    def collective(self, out_t, in_t, fn):
        E = self.eng["pool"]
        self._deps(E, [in_t], [out_t])
        if out_t.sem is None:
            out_t.sem = self.free_sems["hw"].pop()
        ins = fn(E["e"])
        out_t.sem["cnt"] += 1
        ins.then_inc(out_t.sem["sem"], 1)
        tok = (out_t.sem["sem"], out_t.sem["cnt"], "cc")
        self._post(tok, [in_t], [out_t])

    def barrier(self):
        toks = []
        for n, E in self.eng.items():
            if E["cnt"] > 0:
                toks.append((E["sem"], E["cnt"], "x"))
        for st in self.all_dma_sems:
            if st["cnt"] > 0:
                toks.append((st["sem"], st["cnt"], "dma"))
        assert not self.pe_pending
        for n, E in self.eng.items():
            for tok in toks:
                if tok[0] is E["sem"]:
                    continue
                self._wait(E, tok)


XS = {"KTm": (512, NT), "KTr": (32, NT), "Vm": (1024, 1024), "KTd": (512, NT), "Vd": (512, 2048),
      "KTs": (512, NT), "Vs": (1024, 1024)}

QS = {"Qm_s": (768, NT), "Qd_s": (512, NT), "Qs_s": (512, NT)}

W_SHAPES = {
    "w1_a": (NL, D, DFF), "w3_a": (NL, D, DFF), "w2_a": (NL, DFF, D),
    "w1_b": (NL, D, DFF), "w3_b": (NL, D, DFF), "w2_b": (NL, DFF, D),
    "w_in": (NL, D, 3744), "w_kr_s": (NL, D, 32),
    "w_uq_p": (NL, 384, 768), "w_uq_s": (NL, 384, 768), "w_ukv_p": (NL, 256, 1024),
    "w_o_mla": (NL, 512, D), "w_o_diff": (NL, 512, D), "w_o_sb": (NL, 512, D),
    "w_branch_gate": (NL, D, 3 * D), "w_out": (NL, D, D),
    "w_ple_gate": (NL, D, D), "w_ple_proj": (NL, 256, D),
}


def build(plan, fused):
    nc = bass.Bass("TRN2", target_bir_lowering=False)
    steps = [s if isinstance(s, tuple) else (s,) for s in plan]
    kinds = set(s[0] for s in steps)

    def din(name, shape, dt=F32):
        return nc.dram_tensor(name, list(shape), dt, kind="ExternalInput")

    def dout(name, shape, dt=F32):
        return nc.dram_tensor(name, list(shape), dt, kind="ExternalOutput")

    class LazyDR(dict):
        def __missing__(self, k):
            v = din(k, W_SHAPES[k])
            self[k] = v
            TD[k] = G.track(v, k)
            return v

    DR = LazyDR()
    TD = {}
    DR["gvec"] = din("gvec", (128, NGV))
    DR["lamv"] = din("lamv", (128, NL * 4 * 64))
    DR["rope"] = din("rope", (32, 2, NT))
    DR["negm_incl"] = din("negm_incl", (128, 8, 512), BF16)
    DR["negm_strict"] = din("negm_strict", (128, 8, 512), BF16)
    DR["qb"] = din("qb", (4, 4, NT), BF16)
    DR["kb"] = din("kb", (4, 4, 4096), BF16)
    DR["pT"] = din("pT", (NL, 256, NT))
    if "load_x" in kinds:
        DR["xT"] = din("xT", (D, NT))
    if "load_h" in kinds:
        DR["hT_in"] = din("hT_in", (D, NT))
    if "store_h" in kinds:
        DR["hT_out"] = dout("hT_out", (D, NT))
    if "final" in kinds:
        DR["outT"] = dout("outT", (D, NT))
    for k, shp in XS.items():
        if fused:
            DR[k + "_x"] = nc.dram_tensor(k + "_x", list(shp), BF16)
            DR[k + "_g"] = nc.dram_tensor(k + "_g", [2 * shp[0], shp[1]], BF16)
        else:
            if "kv" in kinds:
                DR[k + "_x"] = dout(k + "_x", shp, BF16)
            if "attn" in kinds:
                DR[k + "_g"] = din(k + "_g", (2 * shp[0], shp[1]), BF16)
    for k, shp in QS.items():
        DR[k] = nc.dram_tensor(k, list(shp), BF16)
    dbg = [s for s in steps if s[0] == "dbg"]
    for s in dbg:
        DR[s[1]] = dout(s[1], s[2], F32)

    es = ExitStack()
    with es:
        fw = FW(nc, es)
        with fw.scope() as G:
            for k_, v_ in list(DR.items()):
                TD[k_] = G.track(v_, k_)
            hT = G.sbuf("hT", (128, 8, NT), F32)
            gvec = G.sbuf("gvec_s", (128, NGV), F32)
            ones32 = G.sbuf("ones32", (128, 128), BF16)
            PS = [G.psum("ps%d" % i, (128, 512), F32) for i in range(8)]
            SQ = [G.sbuf("sq%d" % i, (128, 512), BF16) for i in range(2)]
            lnv = G.sbuf("lnv", (128, 512), F32)
            rstd = G.sbuf("rstd", (128, 512), F32)

            fw.dma("sp", gvec, TD["gvec"], gvec[:, :], DR["gvec"][:, :])
            fw.op("dve", lambda e: e.memset(ones32[:, :], 1.0), [], [ones32])

            def mm(out_ap, lhsT_ap, rhs_ap, start, stop, reads, writes, inc=None):
                fw.op("pe", lambda e: e.matmul(out_ap, lhsT_ap, rhs_ap, start=start, stop=stop),
                      reads, writes, inc=(stop if inc is None else inc))

            def wload(dst_t, dst_ap, wname, src_ap, q="pool"):
                fw.dma(q, dst_t, TD[wname], dst_ap, src_ap)

            evac_flip = [0]

            def evac(out_ap, in_ap, reads, writes, eng=None):
                if eng is None:
                    evac_flip[0] ^= 1
                    eng = "act" if evac_flip[0] else "dve"
                if eng == "act":
                    fw.op("act", lambda e: e.activation(out=out_ap, in_=in_ap, func=AF.Copy), reads, writes)
                else:
                    fw.op(eng, lambda e: e.tensor_copy(out=out_ap, in_=in_ap), reads, writes)

            def norm(src_fn, src_tiles, nch, ntok, nfeat, gcol, out_fn, out_tile, bias2=0.0, npart=128):
                st = PS[7]
                for k in range(nch):
                    sq = SQ[k % 2]
                    fw.op("act", lambda e: e.activation(out=sq[:, :ntok], in_=src_fn(k), func=AF.Square),
                          src_tiles, [sq])
                    mm(st[:, :ntok], ones32[:, :], sq[:, :ntok], k == 0, k == nch - 1, [ones32, sq], [st], inc=True)
                fw.op("act", lambda e: e.activation(out=lnv[:, :ntok], in_=st[:, :ntok], func=AF.Ln,
                                                    scale=1.0 / nfeat, bias=EPS), [st], [lnv])
                fw.op("act", lambda e: e.activation(out=rstd[:, :ntok], in_=lnv[:, :ntok], func=AF.Exp,
                                                    scale=-0.5, bias=bias2), [lnv], [rstd])
                for k in range(nch):
                    fw.op("dve", lambda e: e.scalar_tensor_tensor(
                        out=out_fn(k), in0=src_fn(k), scalar=gvec[:, gcol + k:gcol + k + 1],
                        in1=rstd[:, :ntok], op0=ALU.mult, op1=ALU.mult), src_tiles + [gvec, rstd], [out_tile])

            def step_load(name):
                src = DR[name]
                for k in range(8):
                    fw.dma("sp", hT, TD[name], hT[:, k, :], src[k * 128:(k + 1) * 128, :])

            def step_store(name):
                dst = DR[name]
                for k in range(8):
                    fw.dma("sp", TD[name], hT, dst[k * 128:(k + 1) * 128, :], hT[:, k, :])

            def step_ffn(l, which):
                w1n, w3n, w2n = "w1_" + which, "w3_" + which, "w2_" + which
                gcol = l * GL + (0 if which == "a" else 21)
                HB = 1024
                with fw.scope() as S:
                    u = S.sbuf("ffn_u", (128, 8, HB), BF16)
                    gT = S.sbuf("ffn_g", (128, 22, HB), BF16)
                    w1b = [S.sbuf("w1b%d" % i, (128, 8, 256), BF16) for i in range(2)]
                    w3b = [S.sbuf("w3b%d" % i, (128, 8, 256), BF16) for i in range(2)]
                    w2b = [S.sbuf("w2b%d" % i, (128, 22, 128), BF16) for i in range(2)]
                    sil = [S.sbuf("sil%d" % i, (128, 512), F32) for i in range(2)]
                    for half in range(2):
                        t0 = half * HB
                        for tt in range(2):
                            norm(lambda k: hT[:, k, t0 + tt * 512:t0 + (tt + 1) * 512], [hT], 8, 512, D, gcol,
                                 lambda k: u[:, k, tt * 512:(tt + 1) * 512], u)

                        def ld13(gi):
                            c0 = gi * 256
                            wload(w1b[gi % 2], w1b[gi % 2][:, :, :], w1n,
                                  DR[w1n][l, :, c0:c0 + 256].rearrange("(k p) c -> p k c", p=128))
                            wload(w3b[gi % 2], w3b[gi % 2][:, :, :], w3n,
                                  DR[w3n][l, :, c0:c0 + 256].rearrange("(k p) c -> p k c", p=128))

                        def ld2(oc):
                            wload(w2b[oc % 2], w2b[oc % 2][:, :, :], w2n,
                                  DR[w2n][l, :, oc * 128:(oc + 1) * 128].rearrange("(k p) c -> p k c", p=128))

                        ld13(0)
                        cnt = 0
                        for gi in range(11):
                            if gi + 1 < 11:
                                ld13(gi + 1)
                            else:
                                ld2(0)
                            for hh in range(2):
                                hc = gi * 2 + hh
                                for tt in range(2):
                                    p1 = PS[cnt % 2]
                                    p3 = PS[2 + cnt % 2]
                                    sl = sil[cnt % 2]
                                    cnt += 1
                                    for k in range(8):
                                        mm(p1[:, :], w1b[gi % 2][:, k, hh * 128:(hh + 1) * 128],
                                           u[:, k, tt * 512:(tt + 1) * 512], k == 0, k == 7, [w1b[gi % 2], u], [p1])
                                    for k in range(8):
                                        mm(p3[:, :], w3b[gi % 2][:, k, hh * 128:(hh + 1) * 128],
                                           u[:, k, tt * 512:(tt + 1) * 512], k == 0, k == 7, [w3b[gi % 2], u], [p3])
                                    fw.op("act", lambda e: e.activation(out=sl[:, :], in_=p1[:, :], func=AF.Silu),
                                          [p1], [sl])
                                    fw.op("dve", lambda e: e.tensor_tensor(
                                        out=gT[:, hc, tt * 512:(tt + 1) * 512], in0=sl[:, :], in1=p3[:, :],
                                        op=ALU.mult), [sl, p3], [gT])
                        for oc in range(8):
                            if oc + 1 < 8:
                                ld2(oc + 1)
                            for tt in range(2):
                                po = PS[4 + (oc * 2 + tt) % 2]
                                for hc in range(22):
                                    mm(po[:, :], w2b[oc % 2][:, hc, :], gT[:, hc, tt * 512:(tt + 1) * 512],
                                       hc == 0, hc == 21, [w2b[oc % 2], gT], [po])
                                hs = hT[:, oc, t0 + tt * 512:t0 + (tt + 1) * 512]
                                fw.op("dve", lambda e: e.scalar_tensor_tensor(
                                    out=hs, in0=po[:, :], scalar=0.5, in1=hs, op0=ALU.mult, op1=ALU.add),
                                    [po, hT], [hT])

            def step_ple(l):
                gcol = l * GL + 29
                with fw.scope() as S:
                    u = S.sbuf("ple_u", (128, 8, 512), BF16)
                    wg = S.sbuf("ple_wg", (128, 8, D), BF16)
                    wp = S.sbuf("ple_wp", (128, 2, D), BF16)
                    pt = [S.sbuf("ple_p%d" % i, (128, 2, 512), BF16) for i in range(2)]
                    sg = [S.sbuf("ple_sg%d" % i, (128, 512), F32) for i in range(2)]
                    for kk in range(2):
                        wload(wg, wg[:, kk * 4:(kk + 1) * 4, :], "w_ple_gate",
                              DR["w_ple_gate"][l, kk * 512:(kk + 1) * 512, :].rearrange("(k p) c -> p k c", p=128))
                    wload(wp, wp[:, :, :], "w_ple_proj",
                          DR["w_ple_proj"][l, :, :].rearrange("(k p) c -> p k c", p=128))
                    for tt in range(4):
                        ts = slice(tt * 512, (tt + 1) * 512)
                        ptt = pt[tt % 2]
                        wload(ptt, ptt[:, :, :], "pT", DR["pT"][l, :, ts].rearrange("(k p) c -> p k c", p=128))
                        norm(lambda k: hT[:, k, ts], [hT], 8, 512, D, gcol, lambda k: u[:, k, :], u)
                        for oc in range(8):
                            pg = PS[oc % 2]
                            pp = PS[2 + oc % 2]
                            s = sg[oc % 2]
                            for k in range(8):
                                mm(pg[:, :], wg[:, k, oc * 128:(oc + 1) * 128], u[:, k, :], k == 0, k == 7, [wg, u], [pg])
                            for k in range(2):
                                mm(pp[:, :], wp[:, k, oc * 128:(oc + 1) * 128], ptt[:, k, :], k == 0, k == 1, [wp, ptt], [pp])
                            fw.op("act", lambda e: e.activation(out=s[:, :], in_=pg[:, :], func=AF.Sigmoid), [pg], [s])
                            fw.op("dve", lambda e: e.tensor_tensor(out=s[:, :], in0=s[:, :], in1=pp[:, :], op=ALU.mult),
                                  [s, pp], [s])
                            fw.op("dve", lambda e: e.tensor_tensor(out=hT[:, oc, ts], in0=hT[:, oc, ts], in1=s[:, :],
                                                                   op=ALU.add), [s, hT], [hT])

            def step_final():
                with fw.scope() as S:
                    o = [S.sbuf("fin_o%d" % i, (128, 8, 512), F32) for i in range(2)]
                    for tt in range(4):
                        ts = slice(tt * 512, (tt + 1) * 512)
                        ot = o[tt % 2]
                        norm(lambda k: hT[:, k, ts], [hT], 8, 512, D, 2 * GL, lambda k: ot[:, k, :], ot)
                        fw.dma("sp", TD["outT"], ot, DR["outT"][:, ts].rearrange("(k p) c -> p k c", p=128), ot[:, :, :])

            def step_kv(l):
                gb = l * GL
                with fw.scope() as S:
                    u = S.sbuf("kv_u", (128, 8, 512), BF16)
                    wkv = S.sbuf("kv_w", (128, 8, 2368), BF16)
                    wukv = S.sbuf("kv_wukv", (128, 2, 1024), BF16)
                    ckv32 = S.sbuf("kv_ckv32", (128, 2, 512), F32)
                    ckvn = S.sbuf("kv_ckvn", (128, 2, 512), BF16)
                    ropet = S.sbuf("kv_rope", (96, 2, 512), F32)
                    rt = [S.sbuf("kv_rt%d" % i, (96, 512), F32) for i in range(2)]
                    wqa = S.sbuf("kv_wq", (128, 8, 1408), BF16)
                    wuqp_t = S.sbuf("kv_wuqp", (128, 3, 768), BF16)
                    wuqs_t = S.sbuf("kv_wuqs", (128, 3, 768), BF16)
                    cq32 = S.sbuf("kv_cq32", (128, 3, 512), F32)
                    cqn = S.sbuf("kv_cqn", (128, 3, 512), BF16)
                    stg = [S.sbuf("kv_stg%d" % i, (128, 512), BF16) for i in range(4)]
                    sc = [0]

                    def stage():
                        sc[0] += 1
                        return stg[sc[0] % 4]

                    segs = [(0, 384, 256, "w_in"), (256, 640, 32, "w_in"), (288, 0, 32, "w_kr_s"),
                            (320, C_DIFF + 512, 512, "w_in"), (832, C_DIFF + 1024, 512, "w_in"),
                            (1344, C_SB + 512, 512, "w_in"), (1856, C_SB + 1024, 512, "w_in")]
                    for (d0, s0, n, wn) in segs:
                        wload(wkv, wkv[:, :, d0:d0 + n], wn,
                              DR[wn][l, :, s0:s0 + n].rearrange("(k p) c -> p k c", p=128))
                    wload(wukv, wukv[:, :, :], "w_ukv_p", DR["w_ukv_p"][l, :, :].rearrange("(k p) c -> p k c", p=128))
                    for (d0, s0, n) in [(0, 0, 384), (384, C_DIFF, 512), (896, C_SB, 512)]:
                        wload(wqa, wqa[:, :, d0:d0 + n], "w_in", DR["w_in"][l, :, s0:s0 + n].rearrange("(k p) c -> p k c", p=128))
                    wload(wuqp_t, wuqp_t[:, :, :], "w_uq_p", DR["w_uq_p"][l, :, :].rearrange("(k p) c -> p k c", p=128))
                    wload(wuqs_t, wuqs_t[:, :, :], "w_uq_s", DR["w_uq_s"][l, :, :].rearrange("(k p) c -> p k c", p=128))
                    Vm, Vd, Vs = DR["Vm_x"], DR["Vd_x"], DR["Vs_x"]
                    pc = [0]

                    def nps():
                        pc[0] += 1
                        return PS[pc[0] % 6]

                    for tt in range(4):
                        ts = slice(tt * 512, (tt + 1) * 512)
                        norm(lambda k: hT[:, k, ts], [hT], 8, 512, D, gb + 8, lambda k: u[:, k, :], u)
                        fw.dma("sp", ropet, TD["rope"], ropet[0:32, :, :], DR["rope"][:, :, ts])
                        fw.dma("sp", ropet, TD["rope"], ropet[64:96, :, :], DR["rope"][:, :, ts])
                        for ch in range(2):
                            p = nps()
                            for k in range(8):
                                mm(p[:, :], wkv[:, k, ch * 128:(ch + 1) * 128], u[:, k, :], k == 0, k == 7, [wkv, u], [p])
                            evac(ckv32[:, ch, :], p[:, :], [p], [ckv32])
                        norm(lambda k: ckv32[:, k, :], [ckv32], 2, 512, 256, gb + 19, lambda k: ckvn[:, k, :], ckvn)
                        pa = nps()
                        pb = nps()
                        for k in range(8):
                            mm(pa[0:32, :], wkv[:, k, 256:288], u[:, k, :], k == 0, k == 7, [wkv, u], [pa])
                        for k in range(8):
                            mm(pb[0:32, :], wkv[:, k, 288:320], u[:, k, :], k == 0, k == 7, [wkv, u], [pb])
                        fw.op("dve", lambda e: e.tensor_tensor(out=rt[0][0:32, :], in0=pa[0:32, :], in1=ropet[0:32, 0, :],
                                                               op=ALU.mult), [pa, ropet], [rt[0]])
                        fw.op("dve", lambda e: e.tensor_tensor(out=rt[1][0:32, :], in0=pb[0:32, :], in1=ropet[0:32, 1, :],
                                                               op=ALU.mult), [pb, ropet], [rt[1]])
                        sg_ = stage()
                        fw.op("dve", lambda e: e.tensor_tensor(out=sg_[0:32, :], in0=rt[0][0:32, :], in1=rt[1][0:32, :],
                                                               op=ALU.add), [rt[0], rt[1]], [sg_])
                        fw.dma("sp", TD["KTr_x"], sg_, DR["KTr_x"][0:32, ts], sg_[0:32, :])
                        for grp in range(4):
                            p = nps()
                            for k in range(2):
                                mm(p[:, :], wukv[:, k, grp * 128:(grp + 1) * 128], ckvn[:, k, :], k == 0, k == 1,
                                   [wukv, ckvn], [p])
                            sg_ = stage()
                            evac(sg_[:, :], p[:, :], [p], [sg_])
                            fw.dma("sp", TD["KTm_x"], sg_, DR["KTm_x"][grp * 128:(grp + 1) * 128, ts], sg_[:, :])
                        for (c0, kn) in [(320, "KTd_x"), (1344, "KTs_x")]:
                            for grp in range(4):
                                p = nps()
                                for k in range(8):
                                    mm(p[:, :], wkv[:, k, c0 + grp * 128:c0 + (grp + 1) * 128], u[:, k, :],
                                       k == 0, k == 7, [wkv, u], [p])
                                sg_ = stage()
                                evac(sg_[:, :], p[:, :], [p], [sg_])
                                fw.dma("sp", TD[kn], sg_, DR[kn][grp * 128:(grp + 1) * 128, ts], sg_[:, :])
                        for ch in range(3):
                            p = nps()
                            for k in range(8):
                                mm(p[:, :], wqa[:, k, ch * 128:(ch + 1) * 128], u[:, k, :], k == 0, k == 7, [wqa, u], [p])
                            evac(cq32[:, ch, :], p[:, :], [p], [cq32])
                        norm(lambda k: cq32[:, k, :], [cq32], 3, 512, 384, gb + 16, lambda k: cqn[:, k, :], cqn)
                        for h in range(8):
                            pa = nps()
                            pb = nps()
                            for k in range(3):
                                mm(pa[0:96, :], wuqp_t[:, k, h * 96:(h + 1) * 96], cqn[:, k, :], k == 0, k == 2, [wuqp_t, cqn], [pa])
                            for k in range(3):
                                mm(pb[0:96, :], wuqs_t[:, k, h * 96:(h + 1) * 96], cqn[:, k, :], k == 0, k == 2, [wuqs_t, cqn], [pb])
                            fw.op("dve", lambda e: e.tensor_tensor(out=rt[0][64:96, :], in0=pa[64:96, :], in1=ropet[64:96, 0, :], op=ALU.mult),
                                  [pa, ropet], [rt[0]])
                            fw.op("dve", lambda e: e.tensor_tensor(out=rt[1][64:96, :], in0=pb[64:96, :], in1=ropet[64:96, 1, :], op=ALU.mult),
                                  [pb, ropet], [rt[1]])
                            sg_ = stage()
                            fw.op("dve", lambda e: e.tensor_tensor(out=sg_[64:96, :], in0=rt[0][64:96, :], in1=rt[1][64:96, :], op=ALU.add),
                                  [rt[0], rt[1]], [sg_])
                            fw.op("act", lambda e: e.activation(out=sg_[0:64, :], in_=pa[0:64, :], func=AF.Copy), [pa], [sg_])
                            fw.dma("sp", TD["Qm_s"], sg_, DR["Qm_s"][h * 96:(h + 1) * 96, ts], sg_[0:96, :])
                        for (c0, qn) in [(384, "Qd_s"), (896, "Qs_s")]:
                            for grp in range(4):
                                p = nps()
                                for k in range(8):
                                    mm(p[:, :], wqa[:, k, c0 + grp * 128:c0 + (grp + 1) * 128], u[:, k, :],
                                       k == 0, k == 7, [wqa, u], [p])
                                sg_ = stage()
                                evac(sg_[:, :], p[:, :], [p], [sg_])
                                fw.dma("sp", TD[qn], sg_, DR[qn][grp * 128:(grp + 1) * 128, ts], sg_[:, :])
                        for tb in range(4):
                            i = tt * 4 + tb
                            tsl = slice(tb * 128, (tb + 1) * 128)
                            p = nps()
                            for k in range(2):
                                mm(p[:, :], ckvn[:, k, tsl], wukv[:, k, 512:1024], k == 0, k == 1, [wukv, ckvn], [p])
                            sg_ = stage()
                            evac(sg_[:, :], p[:, :], [p], [sg_])
                            fw.dma("sp", TD["Vm_x"], sg_,
                                   Vm.ap().rearrange("(h p) (i d) -> p h i d", p=128, d=64)[:, :, i, :],
                                   sg_[:, :].rearrange("p (h d) -> p h d", d=64))
                            p = nps()
                            for k in range(8):
                                mm(p[:, :], u[:, k, tsl], wkv[:, k, 832:1344], k == 0, k == 7, [wkv, u], [p])
                            sg_ = stage()
                            evac(sg_[:, :], p[:, :], [p], [sg_])
                            fw.dma("sp", TD["Vd_x"], sg_,
                                   Vd.ap().rearrange("(h p) (i d) -> p h i d", p=128, d=128)[:, :, i, :],
                                   sg_[:, :].rearrange("p (h d) -> p h d", d=128))
                            p = nps()
                            for k in range(8):
                                mm(p[:, :], u[:, k, tsl], wkv[:, k, 1856:2368], k == 0, k == 7, [wkv, u], [p])
                            sg_ = stage()
                            evac(sg_[:, :], p[:, :], [p], [sg_])
                            fw.dma("sp", TD["Vs_x"], sg_,
                                   Vs.ap().rearrange("(h p) (i d) -> p h i d", p=128, d=64)[:, :, i, :],
                                   sg_[:, :].rearrange("p (h d) -> p h d", d=64))

            def step_xchg():
                for k in XS:
                    src, dst = DR[k + "_x"], DR[k + "_g"]
                    fw.collective(TD[k + "_g"], TD[k + "_x"], lambda e: e.collective_compute(
                        "AllGather", ALU.bypass, replica_groups=[[0, 1], [2, 3], [4, 5], [6, 7]],
                        ins=[src.ap().opt()], outs=[dst.ap().opt()]))

            def step_attn(l, parts=None):
                gb = l * GL
                lam_init = 0.8 - 0.6 * math.exp(-0.3 * l)
                with fw.scope() as S:
                    u = S.sbuf("at_u", (128, 8, 512), BF16)
                    ymla = S.sbuf("at_ymla", (64, 8, 512), BF16)
                    ysb = S.sbuf("at_ysb", (64, 8, 512), BF16)
                    ydf = S.sbuf("at_ydf", (128, 4, 512), BF16)
                    lams = S.sbuf("at_lams", (128, 4), F32)
                    neglam = S.sbuf("at_neglam", (128, 1), F32)
                    tri = S.sbuf("at_tri", (128, 128), BF16)
                    negones = S.sbuf("at_negones", (128, 128), BF16)
                    ident = S.sbuf("at_ident", (128, 128), BF16)
                    zeros_b = S.sbuf("at_zerosb", (128, 128), BF16)
                    ones_b = S.sbuf("at_onesb", (128, 128), BF16)
                    mask_i = S.sbuf("at_maski", (128, 8, 512), BF16)
                    mask_s = S.sbuf("at_masks", (128, 8, 512), BF16)
                    Qt = [S.sbuf("at_q%d" % i, (128, 512), BF16) for i in range(2)]
                    Kt = [S.sbuf("at_k%d" % i, (128, 2, 16, 128), BF16) for i in range(2)]
                    Vt = [S.sbuf("at_v%d" % i, (128, 2, 16, 128), BF16) for i in range(2)]
                    f32s = [S.sbuf("at_f%d" % i, (128, 512), F32) for i in range(5)]
                    b16s = [S.sbuf("at_b%d" % i, (128, 512), BF16) for i in range(9)]
                    wo = [S.sbuf("ot_wo%d" % i, (128, 20, 128), BF16) for i in range(2)]
                    wg = [S.sbuf("ot_wg%d" % i, (128, 3, 8, 128), BF16) for i in range(2)]
                    wout = S.sbuf("ot_wout", (128, 8, D), BF16)

                    lamt, lamp = f32s[0], f32s[1]
                    fw.dma("sp", lamt, TD["lamv"], lamt[:, 0:256], DR["lamv"][:, l * 256:(l + 1) * 256])
                    for j in range(2):
                        fw.op("dve", lambda e: e.tensor_tensor(out=lamp[:, j * 64:(j + 1) * 64], in0=lamt[:, (2 * j) * 64:(2 * j + 1) * 64],
                                                               in1=lamt[:, (2 * j + 1) * 64:(2 * j + 2) * 64], op=ALU.mult),
                              [lamt], [lamp])
                        fw.op("dve", lambda e: e.reduce_sum(out=lams[:, j:j + 1], in_=lamp[:, j * 64:(j + 1) * 64], axis=AX.X),
                              [lamp], [lams])
                    fw.op("act", lambda e: e.activation(out=lams[:, 2:4], in_=lams[:, 0:2], func=AF.Exp), [lams], [lams])
                    fw.op("dve", lambda e: e.scalar_tensor_tensor(out=neglam[:, :], in0=lams[:, 3:4], scalar=-lam_init,
                                                                  in1=lams[:, 2:3], op0=ALU.add, op1=ALU.subtract),
                          [lams], [neglam])
                    fw.op("pool", lambda e: e.memset(zeros_b[:, :], 0.0), [], [zeros_b])
                    fw.op("pool", lambda e: e.memset(ones_b[:, :], 1.0), [], [ones_b])
                    fw.op("pool", lambda e: e.memset(tri[:, :], 1.0), [], [tri])
                    fw.op("pool", lambda e: e.affine_select(out=ident[:, :], in_=tri[:, :], pattern=[[-1, 128]],
                                                            compare_op=ALU.is_equal, fill=0.0, base=0, channel_multiplier=1),
                          [tri], [ident])
                    fw.op("pool", lambda e: e.memset(negones[:, :], -8.0), [], [negones])
                    fw.op("pool", lambda e: e.affine_select(out=tri[:, :], in_=negones[:, :], pattern=[[-1, 128]],
                                                            compare_op=ALU.is_ge, fill=0.0, base=0, channel_multiplier=1),
                          [negones, ident], [tri])
                    fw.dma("sp", mask_i, TD["negm_incl"], mask_i[:, :, :], DR["negm_incl"][:, :, :])
                    fw.dma("sp", mask_s, TD["negm_strict"], mask_s[:, :, :], DR["negm_strict"][:, :, :])
                    for t_ in Kt + Vt:
                        fw.op("pool", lambda e: e.memset(t_[:, :, :, :], 0.0), [], [t_])
                    for t_ in Qt:
                        fw.op("dve", lambda e: e.memset(t_[:, :], 0.0), [], [t_])
                    for kk in range(2):
                        wload(wout, wout[:, kk * 4:(kk + 1) * 4, :], "w_out",
                              DR["w_out"][l, kk * 512:(kk + 1) * 512, :].rearrange("(k p) c -> p k c", p=128))

                    def ld_o(oc):
                        cs = slice(oc * 128, (oc + 1) * 128)
                        b = wo[oc % 2]
                        wload(b, b[0:64, 0:8, :], "w_o_mla", DR["w_o_mla"][l, :, cs].rearrange("(h p) c -> p h c", p=64))
                        wload(b, b[0:64, 8:16, :], "w_o_sb", DR["w_o_sb"][l, :, cs].rearrange("(h p) c -> p h c", p=64))
                        wload(b, b[:, 16:20, :], "w_o_diff", DR["w_o_diff"][l, :, cs].rearrange("(h p) c -> p h c", p=128))
                        for x in range(3):
                            wload(wg[oc % 2], wg[oc % 2][:, x, :, :], "w_branch_gate",
                                  DR["w_branch_gate"][l, :, x * D + oc * 128:x * D + (oc + 1) * 128]
                                  .rearrange("(k p) c -> p k c", p=128))

                    units = [("mla", h) for h in range(8)] + [("diff", h) for h in range(8)] + [("sb", h) for h in range(8)]

                    for g in range(4):
                        gs = slice(g * 512, (g + 1) * 512)
                        nkb = 8 * g + 8
                        nh = nkb // 2
                        norm(lambda k: hT[:, k, gs], [hT], 8, 512, D, gb + 8, lambda k: u[:, k, :], u)

                        def load_k(kt, rows, kn, r0, d0=0):
                            R_ = XS[kn][0]
                            for r in range(2):
                                fw.dma("sp", kt, TD[kn + "_g"], kt[d0:d0 + rows, r, 0:nh, :],
                                       DR[kn + "_g"][r * R_ + r0:r * R_ + r0 + rows, 0:nh * 128]
                                       .rearrange("d (i k) -> d i k", k=128))

                        def load_v(vt, name, h, dv):
                            src = DR[name].ap().rearrange("(r h p) (i d) -> r h p i d", r=2, p=128, d=dv)
                            for r in range(2):
                                fw.dma("sp", vt, TD[name], vt[:, r, 0:nh, 0:dv], src[r, h, :, 0:nh, :])

                        def load_unit(idx):
                            kind, h = units[idx]
                            kt, qt = Kt[idx % 2], Qt[idx % 2]
                            if kind == "mla":
                                load_k(kt, 64, "KTm", h * 64, 0)
                                load_k(kt, 32, "KTr", 0, 64)
                                load_v(Vt[h % 2], "Vm_g", h, 64)
                                fw.dma("sp", qt, TD["Qm_s"], qt[0:96, :], DR["Qm_s"][h * 96:(h + 1) * 96, gs])
                            elif kind == "diff":
                                load_k(kt, 64, "KTd", h * 64, 0)
                                fw.dma("sp", kt, TD["kb"], kt[64:68, :, :, :],
                                       DR["kb"][h // 2, :, :].rearrange("d (r i k) -> d r i k", r=2, k=128))
                                if h % 2 == 0:
                                    load_v(Vt[(h // 2) % 2], "Vd_g", h // 2, 128)
                                if h < 2:
                                    fw.op("dve", lambda e: e.memset(qt[64:128, :], 0.0), [], [qt])
                                fw.dma("sp", qt, TD["Qd_s"], qt[0:64, :], DR["Qd_s"][h * 64:(h + 1) * 64, gs])
                                fw.dma("sp", qt, TD["qb"], qt[64:68, :], DR["qb"][h // 2, :, gs])
                            else:
                                load_k(kt, 64, "KTs", h * 64, 0)
                                load_v(Vt[h % 2], "Vs_g", h, 64)
                                if h < 2:
                                    fw.op("dve", lambda e: e.memset(qt[64:128, :], 0.0), [], [qt])
                                fw.dma("sp", qt, TD["Qs_s"], qt[0:64, :], DR["Qs_s"][h * 64:(h + 1) * 64, gs])

                        def kblk(kt, j):
                            return kt[:, j % 2, j // 2, :]

                        def vblk(vt, j):
                            return vt[:, j % 2, j // 2, :]

                        def q0_of(j):
                            return ((j - 8 * g) // 2) * 128 if j >= 8 * g else 0

                        def softmax_loop(kt, vt, q_t, scale, po, pd):
                            LA = 2

                            def qk(j):
                                q0 = q0_of(j)
                                dg = j >= 8 * g
                                mm(PS[j % 4][:, q0:], kblk(kt, j), q_t[:, q0:], True, not dg, [kt, q_t], [PS[j % 4]])
                                if dg:
                                    mm(PS[j % 4][:, q0:], ident[:, :], mask_i[:, j - 8 * g, q0:], False, True, [ident, mask_i], [PS[j % 4]])

                            for j in range(min(LA, nkb)):
                                qk(j)
                            for j in range(nkb):
                                pS = PS[j % 4]
                                P = b16s[j % 4]
                                q0 = q0_of(j)
                                fw.op("act", lambda e: e.activation(out=P[:, q0:], in_=pS[:, q0:], func=AF.Exp, scale=scale), [pS], [P])
                                if j + LA < nkb:
                                    qk(j + LA)
                                mm(po[:, q0:], vblk(vt, j), P[:, q0:], j == 0, j == nkb - 1, [vt, P], [po])
                                mm(pd[:, q0:], ones_b[:, :], P[:, q0:], j == 0, j == nkb - 1, [ones_b, P], [pd])

                        def recip_act(dst, src_ps, npart):
                            fw.op("act", lambda e: e.activation(out=dst[0:npart, :], in_=src_ps[0:npart, :], func=AF.Ln), [src_ps], [dst])
                            fw.op("act", lambda e: e.activation(out=dst[0:npart, :], in_=dst[0:npart, :], func=AF.Exp, scale=-1.0), [dst], [dst])

                        def unit_mla(idx, h):
                            kt, vt, qt = Kt[idx % 2], Vt[h % 2], Qt[idx % 2]
                            po, pd = PS[4 + h % 2], PS[6 + h % 2]
                            softmax_loop(kt, vt, qt, 96 ** -0.5, po, pd)
                            rc = f32s[2 + h % 2]
                            recip_act(rc, pd, 64)
                            fw.op("dve", lambda e: e.tensor_tensor(out=ymla[:, h, :], in0=po[0:64, :], in1=rc[0:64, :], op=ALU.mult),
                                  [po, rc], [ymla])

                        def unit_diff(idx, hc):
                            h, c = hc // 2, hc % 2
                            kt, vt, qt = Kt[idx % 2], Vt[h % 2], Qt[idx % 2]
                            po, pdn = PS[4 + c], PS[6 + c]
                            softmax_loop(kt, vt, qt, 0.125, po, pdn)
                            r0, r1, a0, a1, od = f32s[0], f32s[1], f32s[2], f32s[3], f32s[4]
                            if c == 0:
                                recip_act(r0, PS[6], 128)
                                fw.op("dve", lambda e: e.tensor_tensor(out=a0[:, :], in0=PS[4][:, :], in1=r0[:, :], op=ALU.mult), [PS[4], r0], [a0])
                            if c == 1:
                                recip_act(r1, PS[7], 128)
                                fw.op("dve", lambda e: e.tensor_tensor(out=a1[:, :], in0=PS[5][:, :], in1=r1[:, :], op=ALU.mult), [PS[5], r1], [a1])
                                fw.op("dve", lambda e: e.scalar_tensor_tensor(out=od[:, :], in0=a1[:, :], scalar=neglam[:, 0:1], in1=a0[:, :],
                                                                              op0=ALU.mult, op1=ALU.add), [a1, a0, neglam], [od])
                                norm(lambda k: od[:, :], [od], 1, 512, 128, gb + 37, lambda k: ydf[:, h, :], ydf,
                                     bias2=math.log(1.0 - lam_init))

                        def unit_sb(idx, h):
                            kt, vt, qh = Kt[idx % 2], Vt[h % 2], Qt[idx % 2]
                            po = PS[6 + h % 2]
                            acc32 = f32s[3]
                            ee = [f32s[0], f32s[1], f32s[2]]
                            spb = [b16s[0], b16s[1], b16s[2]]
                            accb = [b16s[3], b16s[4], b16s[5], b16s[6]]
                            wb = [b16s[7], b16s[8]]
                            order = list(range(nkb - 1, -1, -1))
                            fw.op("dve", lambda e: e.memset(acc32[:, :], 0.0), [], [acc32])
                            for ab_ in accb:
                                fw.op("dve", lambda e: e.memset(ab_[:, :], 0.0), [], [ab_])
                            mm(po[:, :], zeros_b[:, :], mask_s[:, 0, :], True, False, [zeros_b, mask_s], [po])

                            def stage1(jj):
                                j = order[jj]
                                q0 = q0_of(j)
                                pA = PS[jj % 3]
                                e_ = ee[jj % 3]
                                sp = spb[jj % 3]
                                dg = j >= 8 * g
                                mm(pA[:, q0:], kblk(kt, j), qh[:, q0:], True, not dg, [kt, qh], [pA])
                                if dg:
                                    mm(pA[:, q0:], ident[:, :], mask_s[:, j - 8 * g, q0:], False, True, [ident, mask_s], [pA])
                                fw.op("act", lambda e: e.activation(out=e_[:, q0:], in_=pA[:, q0:], func=AF.Exp, scale=0.125), [pA], [e_])
                                fw.op("act", lambda e: e.activation(out=sp[:, q0:], in_=e_[:, q0:], func=AF.Ln, bias=1.0), [e_], [sp])
                                if jj + 1 < nkb:
                                    ab = accb[jj % 4]
                                    fw.op("dve", lambda e: e.tensor_tensor(out=ab[:, q0:], in0=acc32[:, q0:], in1=sp[:, q0:], op=ALU.add),
                                          [acc32, sp], [ab])
                                    if jj + 2 < nkb:
                                        fw.op("dve", lambda e: e.tensor_tensor(out=acc32[:, q0:], in0=acc32[:, q0:], in1=sp[:, q0:], op=ALU.add),
                                              [acc32, sp], [acc32])

                            def stage2a(jj):
                                j = order[jj]
                                q0 = q0_of(j)
                                pX = PS[3 + jj % 2]
                                sp = spb[jj % 3]
                                w = wb[jj % 2]
                                mm(pX[:, q0:], kblk(kt, j), qh[:, q0:], True, False, [kt, qh], [pX])
                                if j >= 8 * g:
                                    mm(pX[:, q0:], ident[:, :], mask_s[:, j - 8 * g, q0:], False, False, [ident, mask_s], [pX])
                                mm(pX[:, q0:], tri[:, :], sp[:, q0:], False, jj == 0, [tri, sp], [pX])
                                if jj > 0:
                                    cb = accb[(jj - 1) % 4]
                                    mm(pX[:, q0:], negones[:, :], cb[:, q0:], False, True, [negones, cb], [pX])
                                fw.op("act", lambda e: e.activation(out=w[:, q0:], in_=pX[:, q0:], func=AF.Exp, scale=0.125), [pX], [w])

                            def stage2b(jj):
                                j = order[jj]
                                q0 = q0_of(j)
                                w = wb[jj % 2]
                                mm(po[:, q0:], vblk(vt, j), w[:, q0:], False, jj == nkb - 1, [vt, w], [po])

                            stage1(0)
                            if nkb > 1:
                                stage1(1)
                            for jj in range(nkb):
                                stage2a(jj)
                                if jj + 2 < nkb:
                                    stage1(jj + 2)
                                stage2b(jj)
                            evac(ysb[:, h, :], po[0:64, :], [po], [ysb])

                        load_unit(0)
                        for idx, (kind, h) in enumerate(units):
                            if idx + 1 < len(units):
                                load_unit(idx + 1)
                            if idx == 18:
                                ld_o(0)
                            if kind == "mla":
                                unit_mla(idx, h)
                            elif kind == "diff":
                                unit_diff(idx, h)
                            else:
                                unit_sb(idx, h)

                        sgt = [f32s[0], f32s[1], f32s[2]]
                        acc = f32s[4]
                        for oc in range(8):
                            if oc + 1 < 8:
                                ld_o(oc + 1)
                            b = wo[oc % 2]
                            g_ = wg[oc % 2]
                            pA, pB, pC = PS[0], PS[1], PS[2]
                            for h in range(8):
                                mm(pA[:, :], b[0:64, h, :], ymla[:, h, :], h == 0, h == 7, [b, ymla], [pA])
                            for h in range(4):
                                mm(pB[:, :], b[:, 16 + h, :], ydf[:, h, :], h == 0, h == 3, [b, ydf], [pB])
                            for h in range(8):
                                mm(pC[:, :], b[0:64, 8 + h, :], ysb[:, h, :], h == 0, h == 7, [b, ysb], [pC])
                            for x in range(3):
                                pg = PS[3 + x]
                                for k in range(8):
                                    mm(pg[:, :], g_[:, x, k, :], u[:, k, :], k == 0, k == 7, [g_, u], [pg])
                                fw.op("act", lambda e: e.activation(out=sgt[x][:, :], in_=pg[:, :], func=AF.Sigmoid), [pg], [sgt[x]])
                            fw.op("dve", lambda e: e.tensor_tensor(out=acc[:, :], in0=sgt[0][:, :], in1=pA[:, :], op=ALU.mult),
                                  [sgt[0], pA], [acc])
                            fw.op("dve", lambda e: e.tensor_tensor(out=sgt[1][:, :], in0=sgt[1][:, :], in1=pB[:, :], op=ALU.mult),
                                  [sgt[1], pB], [sgt[1]])
                            fw.op("dve", lambda e: e.tensor_tensor(out=sgt[2][:, :], in0=sgt[2][:, :], in1=pC[:, :], op=ALU.mult),
                                  [sgt[2], pC], [sgt[2]])
                            fw.op("dve", lambda e: e.tensor_tensor(out=acc[:, :], in0=acc[:, :], in1=sgt[1][:, :], op=ALU.add),
                                  [acc, sgt[1]], [acc])
                            fw.op("dve", lambda e: e.tensor_tensor(out=b16s[oc][:, :], in0=acc[:, :], in1=sgt[2][:, :], op=ALU.add),
                                  [acc, sgt[2]], [b16s[oc]])
                        for oc in range(8):
                            p = PS[6 + oc % 2]
                            for k in range(8):
                                mm(p[:, :], wout[:, k, oc * 128:(oc + 1) * 128], b16s[k][:, :], k == 0, k == 7, [wout, b16s[k]], [p])
                            fw.op("dve", lambda e: e.tensor_tensor(out=hT[:, oc, gs], in0=hT[:, oc, gs], in1=p[:, :], op=ALU.add),
                                  [hT, p], [hT])

            for s in steps:
                k = s[0]
                if k == "load_x":
                    step_load("xT")
                elif k == "load_h":
                    step_load("hT_in")
                elif k == "store_h":
                    step_store("hT_out")
                elif k == "ffn":
                    step_ffn(s[1], s[2])
                elif k == "ple":
                    step_ple(s[1])
                elif k == "kv":
                    step_kv(s[1])
                elif k == "xchg":
                    step_xchg()
                elif k == "attn":
                    step_attn(s[1], *s[2:])
                elif k == "final":
                    step_final()
                elif k == "dbg":
                    pass
                else:
                    raise ValueError(k)
            fw.barrier()
        print("bass program: n_ins=%d n_wait=%d" % (fw.n_ins, fw.n_wait))
    nc.declared_inputs = set(DR.keys())
    return nc


def own_pos(c):
    return ((2 * np.arange(16)[:, None] + c) * 128 + np.arange(128)[None, :]).reshape(-1)


def prep_common(inp):
    f = lambda a: np.ascontiguousarray(np.asarray(a, dtype=np.float32))
    W = {}
    for k in ["w1_a", "w3_a", "w2_a", "w1_b", "w3_b", "w2_b", "w_in", "w_o_mla", "w_o_diff", "w_o_sb",
              "w_branch_gate", "w_out", "w_ple_gate", "w_ple_proj"]:
        W[k] = f(inp[k])
    w_in = W["w_in"]
    kr = w_in[:, :, 640:672]
    W["w_kr_s"] = f(np.concatenate([kr[:, :, 16:32], kr[:, :, 0:16]], axis=2))
    wuq = f(inp["w_uq"]).reshape(NL, 384, 8, 96)
    nope, ropec = wuq[..., :64], wuq[..., 64:]
    W["w_uq_p"] = f(wuq.reshape(NL, 384, 768))
    W["w_uq_s"] = f(np.concatenate([np.zeros_like(nope), ropec[..., 16:], ropec[..., :16]], axis=3).reshape(NL, 384, 768))
    wukv = f(inp["w_ukv"]).reshape(NL, 256, 8, 128)
    W["w_ukv_p"] = f(np.concatenate([wukv[..., :64].reshape(NL, 256, 512), wukv[..., 64:].reshape(NL, 256, 512)], axis=2))
    gv = np.zeros((128, NGV), np.float32)

    def put(col, vec):
        v = np.asarray(vec, np.float32).reshape(-1, 128)
        gv[:, col:col + v.shape[0]] = v.T

    for l in range(NL):
        b = l * GL
        put(b + 0, inp["g_ffn1"][l]); put(b + 8, inp["g_mix"][l]); put(b + 16, inp["g_cq"][l])
        put(b + 19, inp["g_ckv"][l]); put(b + 21, inp["g_ffn2"][l]); put(b + 29, inp["g_ple"][l])
        put(b + 37, inp["g_subln"][l])
    put(2 * GL, inp["g_final"])
    W["gvec"] = gv
    lam = np.stack([np.asarray(inp[k], np.float32) for k in ["lambda_q1", "lambda_k1", "lambda_q2", "lambda_k2"]], axis=1)
    W["lamv"] = f(np.broadcast_to(lam.reshape(1, NL * 4 * 64), (128, NL * 4 * 64)))
    return W


def prep_core(inp, core, need):
    b, c = core // 2, core % 2
    pos = own_pos(c)
    bf = ml_dtypes.bfloat16
    m = {}
    if "xT" in need:
        m["xT"] = np.ascontiguousarray(np.asarray(inp["x"], np.float32)[b][pos].T)
    m["pT"] = np.ascontiguousarray(np.asarray(inp["p"], np.float32)[:, b][:, pos].transpose(0, 2, 1))
    half = 16
    freqs = (1.0 / (10000.0 ** (np.arange(half, dtype=np.float32) / half))).astype(np.float32)
    ang = pos.astype(np.float32)[:, None] * freqs[None, :]
    cs, sn = np.cos(ang).astype(np.float32), np.sin(ang).astype(np.float32)
    rope = np.zeros((32, 2, NT), np.float32)
    rope[:16, 0] = cs.T; rope[16:, 0] = cs.T
    rope[:16, 1] = -sn.T; rope[16:, 1] = sn.T
    m["rope"] = rope
    kp = np.arange(128)[:, None, None]
    jj = np.arange(8)[None, :, None]
    q = np.arange(512)[None, None, :]
    kpos = jj * 128 + kp
    qpos = (c + 2 * (q // 128)) * 128 + q % 128
    m["negm_incl"] = np.where(kpos <= qpos, 0.0, -30000.0).astype(np.float32).astype(bf)
    m["negm_strict"] = np.where(kpos < qpos, 0.0, -30000.0).astype(np.float32).astype(bf)
    qb = np.zeros((4, 4, NT), np.float32)
    kb = np.zeros((4, 4, 2, 16, 128), np.float32)
    gk = ((2 * np.arange(16)[None, :, None] + np.arange(2)[:, None, None]) * 128 + np.arange(128)[None, None, :])
    for h in range(4):
        s8 = 8.0 * 2.0 ** (-2.0 * (h + 1))
        qb[h, 0] = 1.0; qb[h, 1] = 1.0
        qb[h, 2] = -(pos // 64 * 64) * s8; qb[h, 3] = -(pos % 64) * s8
        kb[h, 0] = (gk // 64 * 64) * s8; kb[h, 1] = (gk % 64) * s8
        kb[h, 2] = 1.0; kb[h, 3] = 1.0
    m["qb"] = qb.astype(bf)
    m["kb"] = kb.reshape(4, 4, 4096).astype(bf)
    return m


LAYER = lambda l: [("ffn", l, "a"), ("kv", l), ("xchg",), ("attn", l), ("ffn", l, "b"), ("ple", l)]
PLAN_FUSED = ["load_x"] + LAYER(0) + LAYER(1) + ["final"]


def run(nc, maps):
    maps = [{k: v for k, v in m.items() if k in nc.declared_inputs} for m in maps]
    res = run_bass_kernel_spmd(nc, maps, core_ids=list(range(8)))
    return res.results


def kernel(**inputs):
    W = prep_common(inputs)
    cores = [prep_core(inputs, c, ["xT"]) for c in range(8)]
    nc = build(PLAN_FUSED, True)
    res = run(nc, [dict(W, **cores[c]) for c in range(8)])
    out = np.zeros((4, 4096, D), np.float32)
    for c in range(8):
        out[c // 2][own_pos(c % 2)] = res[c]["outT"].T
    return out
```

```python
import math
from contextlib import ExitStack

import ml_dtypes
import numpy as np

import concourse.bass as bass
import concourse.mybir as mybir
from concourse.bass_utils import run_bass_kernel_spmd

F32 = mybir.dt.float32
BF16 = mybir.dt.bfloat16
AF = mybir.ActivationFunctionType
ALU = mybir.AluOpType
AX = mybir.AxisListType

D = 1024
NT = 2048
DFF = 2816
NL = 2
EPS = 1e-6
KT_ROWS = 1568
GL = 38
NGV = 2 * GL + 8
C_DIFF = 672
C_SB = 2208


class T:
    def __init__(self, h, name):
        self.h = h
        self.name = name
        self.lw = None
        self.lwx = {}
        self.rd = []
        self.sem = None

    def __getitem__(self, idx):
        return self.h[idx]


class Scope:
    def __init__(self, fw):
        self.fw = fw
        self.es = ExitStack()
        self.tiles = []

    def __enter__(self):
        self.es.__enter__()
        return self

    def __exit__(self, *a):
        self.fw.barrier()
        for t in self.tiles:
            if t.sem is not None:
                self.fw.free_sems[t.sem["kind"]].append(t.sem)
                t.sem = None
        return self.es.__exit__(*a)

    def sbuf(self, name, shape, dtype):
        self.fw.uid += 1
        name = "%s_%d" % (name, self.fw.uid)
        h = self.es.enter_context(self.fw.nc.sbuf_tensor(name, list(shape), dtype))
        t = T(h, name)
        self.tiles.append(t)
        return t

    def psum(self, name, shape, dtype):
        h = self.es.enter_context(self.fw.nc.psum_tensor(name, list(shape), dtype))
        t = T(h, name)
        self.tiles.append(t)
        return t

    def track(self, h, name):
        t = T(h, name)
        self.tiles.append(t)
        return t


class FW:
    def __init__(self, nc, es, n_dma_sems=56):
        self.nc = nc
        self.eng = {}
        for name, e in [("pe", nc.tensor), ("act", nc.scalar), ("dve", nc.vector),
                        ("pool", nc.gpsimd), ("sp", nc.sync)]:
            sem = es.enter_context(nc.semaphore("s_" + name))
            self.eng[name] = dict(e=e, sem=sem, cnt=0, known={}, name=name)
        self.pe_pending = []
        self.free_sems = {"sw": [], "hw": []}
        self.all_dma_sems = []
        for i in range(n_dma_sems):
            s = es.enter_context(nc.semaphore("d%d" % i))
            kind = "sw" if i % 3 == 0 else "hw"
            st = dict(sem=s, cnt=0, kind=kind)
            self.free_sems[kind].append(st)
            self.all_dma_sems.append(st)
        self.n_wait = 0
        self.n_ins = 0
        self.uid = 0

    def scope(self):
        return Scope(self)

    def _wait(self, E, tok):
        if tok is None:
            return
        sem, val, kind = tok
        if kind == "pe" and E["name"] == "pe":
            return
        key = id(sem)
        if E["known"].get(key, 0) >= val:
            return
        E["e"].wait_ge(sem, val)
        E["known"][key] = val
        self.n_wait += 1

    def _deps(self, E, reads, writes, dma_out=None, store=False):
        if E["name"] != "pe" and self.pe_pending:
            for t in writes:
                for r, w in self.pe_pending:
                    assert all(t is not x for x in r) and all(t is not x for x in w), "pending PE access to " + t.name
            for t in reads:
                for r, w in self.pe_pending:
                    assert all(t is not x for x in w), "pending PE write to " + t.name
        for t in reads:
            self._wait(E, t.lw)
            for tok in t.lwx.values():
                self._wait(E, tok)
        for t in writes:
            own = (dma_out is not None and t is dma_out and t.lw is not None and t.lw[2] == "dma"
                   and t.sem is not None and t.lw[0] is t.sem["sem"])
            if not own:
                self._wait(E, t.lw)
            if not (store and t is dma_out):
                for tok in t.lwx.values():
                    self._wait(E, tok)
            for tok in t.rd:
                self._wait(E, tok)

    def _post(self, tok, reads, writes):
        for t in reads:
            t.rd.append(tok)
            if len(t.rd) > 16:
                d = {}
                for k in t.rd:
                    kk = id(k[0])
                    if kk not in d or d[kk][1] < k[1]:
                        d[kk] = k
                t.rd = list(d.values())
        for t in writes:
            t.lw = tok
            t.rd = []

    def op(self, engname, fn, reads=(), writes=(), inc=True):
        E = self.eng[engname]
        self._deps(E, reads, writes)
        ins = fn(E["e"])
        self.n_ins += 1
        if engname == "pe" and not inc:
            self.pe_pending.append((list(reads), list(writes)))
            return ins
        E["cnt"] += 1
        ins.then_inc(E["sem"], 1)
        tok = (E["sem"], E["cnt"], engname)
        self._post(tok, reads, writes)
        if engname == "pe":
            for r, w in self.pe_pending:
                self._post(tok, r, w)
            self.pe_pending = []
        return ins

    def dma(self, qname, out_t, in_t, out_ap, in_ap, store=False, **kw):
        E = self.eng[qname]
        self._deps(E, [in_t], [out_t], dma_out=out_t, store=store)
        kind = "sw" if qname == "pool" else "hw"
        owner = in_t if store else out_t
        if owner.sem is None:
            owner.sem = self.free_sems[kind].pop()
        assert owner.sem["kind"] == kind, "tile %s used by both SW and HW DGE" % owner.name
        ins = E["e"].dma_start(out=out_ap, in_=in_ap, **kw)
        self.n_ins += 1
        owner.sem["cnt"] += 16
        ins.then_inc(owner.sem["sem"], 16)
        tok = (owner.sem["sem"], owner.sem["cnt"], "dma")
        if store:
            in_t.rd.append(tok)
            out_t.lwx[id(owner.sem["sem"])] = tok
        else:
            self._post(tok, [in_t], [out_t])
        return ins

    def collective(self, out_t, in_t, fn):
        E = self.eng["pool"]
        self._deps(E, [in_t], [out_t])
        if out_t.sem is None:
            out_t.sem = self.free_sems["hw"].pop()
        ins = fn(E["e"])
        out_t.sem["cnt"] += 1
        ins.then_inc(out_t.sem["sem"], 1)
        tok = (out_t.sem["sem"], out_t.sem["cnt"], "cc")
        self._post(tok, [in_t], [out_t])

    def barrier(self):
        toks = []
        for n, E in self.eng.items():
            if E["cnt"] > 0:
                toks.append((E["sem"], E["cnt"], "x"))
        for st in self.all_dma_sems:
            if st["cnt"] > 0:
                toks.append((st["sem"], st["cnt"], "dma"))
        assert not self.pe_pending
        for n, E in self.eng.items():
            for tok in toks:
                if tok[0] is E["sem"]:
                    continue
                self._wait(E, tok)


XS = {"KTm": (512, NT), "KTr": (32, NT), "Vm": (1024, 1024), "KTd": (512, NT), "Vd": (512, 2048),
      "KTs": (512, NT), "Vs": (1024, 1024)}

QS = {"Qm_s": (768, NT), "Qd_s": (512, NT), "Qs_s": (512, NT)}

W_SHAPES = {
    "w1_a": (NL, D, DFF), "w3_a": (NL, D, DFF), "w2_a": (NL, DFF, D),
    "w1_b": (NL, D, DFF), "w3_b": (NL, D, DFF), "w2_b": (NL, DFF, D),
    "w_in": (NL, D, 3744), "w_kr_s": (NL, D, 32),
    "w_uq_p": (NL, 384, 768), "w_uq_s": (NL, 384, 768), "w_ukv_p": (NL, 256, 1024),
    "w_o_mla": (NL, 512, D), "w_o_diff": (NL, 512, D), "w_o_sb": (NL, 512, D),
    "w_branch_gate": (NL, D, 3 * D), "w_out": (NL, D, D),
    "w_ple_gate": (NL, D, D), "w_ple_proj": (NL, 256, D),
}


def build(plan, fused):
    nc = bass.Bass("TRN2", target_bir_lowering=False)
    steps = [s if isinstance(s, tuple) else (s,) for s in plan]
    kinds = set(s[0] for s in steps)

    def din(name, shape, dt=F32):
        return nc.dram_tensor(name, list(shape), dt, kind="ExternalInput")

    def dout(name, shape, dt=F32):
        return nc.dram_tensor(name, list(shape), dt, kind="ExternalOutput")

    class LazyDR(dict):
        def __missing__(self, k):
            v = din(k, W_SHAPES[k])
            self[k] = v
            TD[k] = G.track(v, k)
            return v

    DR = LazyDR()
    TD = {}
    DR["gvec"] = din("gvec", (128, NGV))
    DR["lamv"] = din("lamv", (128, NL * 4 * 64))
    DR["rope"] = din("rope", (32, 2, NT))
    DR["negm_incl"] = din("negm_incl", (128, 8, 512), BF16)
    DR["negm_strict"] = din("negm_strict", (128, 8, 512), BF16)
    DR["qb"] = din("qb", (4, 4, NT), BF16)
    DR["kb"] = din("kb", (4, 4, 4096), BF16)
    DR["pT"] = din("pT", (NL, 256, NT))
    if "load_x" in kinds:
        DR["xT"] = din("xT", (D, NT))
    if "load_h" in kinds:
        DR["hT_in"] = din("hT_in", (D, NT))
    if "store_h" in kinds:
        DR["hT_out"] = dout("hT_out", (D, NT))
    if "final" in kinds:
        DR["outT"] = dout("outT", (D, NT))
    for k, shp in XS.items():
        if fused:
            DR[k + "_x"] = nc.dram_tensor(k + "_x", list(shp), BF16)
            DR[k + "_g"] = nc.dram_tensor(k + "_g", [2 * shp[0], shp[1]], BF16)
        else:
            if "kv" in kinds:
                DR[k + "_x"] = dout(k + "_x", shp, BF16)
            if "attn" in kinds:
                DR[k + "_g"] = din(k + "_g", (2 * shp[0], shp[1]), BF16)
    for k, shp in QS.items():
        DR[k] = nc.dram_tensor(k, list(shp), BF16)
    dbg = [s for s in steps if s[0] == "dbg"]
    for s in dbg:
        DR[s[1]] = dout(s[1], s[2], F32)

    es = ExitStack()
    with es:
        fw = FW(nc, es)
        with fw.scope() as G:
            for k_, v_ in list(DR.items()):
                TD[k_] = G.track(v_, k_)
            hT = G.sbuf("hT", (128, 8, NT), F32)
            gvec = G.sbuf("gvec_s", (128, NGV), F32)
            ones32 = G.sbuf("ones32", (128, 128), BF16)
            PS = [G.psum("ps%d" % i, (128, 512), F32) for i in range(8)]
            SQ = [G.sbuf("sq%d" % i, (128, 512), BF16) for i in range(2)]
            lnv = G.sbuf("lnv", (128, 512), F32)
            rstd = G.sbuf("rstd", (128, 512), F32)

            fw.dma("sp", gvec, TD["gvec"], gvec[:, :], DR["gvec"][:, :])
            fw.op("dve", lambda e: e.memset(ones32[:, :], 1.0), [], [ones32])

            def mm(out_ap, lhsT_ap, rhs_ap, start, stop, reads, writes, inc=None):
                fw.op("pe", lambda e: e.matmul(out_ap, lhsT_ap, rhs_ap, start=start, stop=stop),
                      reads, writes, inc=(stop if inc is None else inc))

            def wload(dst_t, dst_ap, wname, src_ap, q="pool"):
                fw.dma(q, dst_t, TD[wname], dst_ap, src_ap)

            evac_flip = [0]

            def evac(out_ap, in_ap, reads, writes, eng=None):
                if eng is None:
                    evac_flip[0] ^= 1
                    eng = "act" if evac_flip[0] else "dve"
                if eng == "act":
                    fw.op("act", lambda e: e.activation(out=out_ap, in_=in_ap, func=AF.Copy), reads, writes)
                else:
                    fw.op(eng, lambda e: e.tensor_copy(out=out_ap, in_=in_ap), reads, writes)

            def norm(src_fn, src_tiles, nch, ntok, nfeat, gcol, out_fn, out_tile, bias2=0.0, npart=128):
                st = PS[7]
                for k in range(nch):
                    sq = SQ[k % 2]
                    fw.op("act", lambda e: e.activation(out=sq[:, :ntok], in_=src_fn(k), func=AF.Square),
                          src_tiles, [sq])
                    mm(st[:, :ntok], ones32[:, :], sq[:, :ntok], k == 0, k == nch - 1, [ones32, sq], [st], inc=True)
                fw.op("act", lambda e: e.activation(out=lnv[:, :ntok], in_=st[:, :ntok], func=AF.Ln,
                                                    scale=1.0 / nfeat, bias=EPS), [st], [lnv])
                fw.op("act", lambda e: e.activation(out=rstd[:, :ntok], in_=lnv[:, :ntok], func=AF.Exp,
                                                    scale=-0.5, bias=bias2), [lnv], [rstd])
                for k in range(nch):
                    fw.op("dve", lambda e: e.scalar_tensor_tensor(
                        out=out_fn(k), in0=src_fn(k), scalar=gvec[:, gcol + k:gcol + k + 1],
                        in1=rstd[:, :ntok], op0=ALU.mult, op1=ALU.mult), src_tiles + [gvec, rstd], [out_tile])

            def step_load(name):
                src = DR[name]
                for k in range(8):
                    fw.dma("sp", hT, TD[name], hT[:, k, :], src[k * 128:(k + 1) * 128, :])

            def step_store(name):
                dst = DR[name]
                for k in range(8):
                    fw.dma("sp", TD[name], hT, dst[k * 128:(k + 1) * 128, :], hT[:, k, :])

            def step_ffn(l, which):
                w1n, w3n, w2n = "w1_" + which, "w3_" + which, "w2_" + which
                gcol = l * GL + (0 if which == "a" else 21)
                HB = 1024
                with fw.scope() as S:
                    u = S.sbuf("ffn_u", (128, 8, HB), BF16)
                    gT = S.sbuf("ffn_g", (128, 22, HB), BF16)
                    w1b = [S.sbuf("w1b%d" % i, (128, 8, 256), BF16) for i in range(2)]
                    w3b = [S.sbuf("w3b%d" % i, (128, 8, 256), BF16) for i in range(2)]
                    w2b = [S.sbuf("w2b%d" % i, (128, 22, 128), BF16) for i in range(2)]
                    sil = [S.sbuf("sil%d" % i, (128, 512), F32) for i in range(2)]
                    for half in range(2):
                        t0 = half * HB
                        for tt in range(2):
                            norm(lambda k: hT[:, k, t0 + tt * 512:t0 + (tt + 1) * 512], [hT], 8, 512, D, gcol,
                                 lambda k: u[:, k, tt * 512:(tt + 1) * 512], u)

                        def ld13(gi):
                            c0 = gi * 256
                            wload(w1b[gi % 2], w1b[gi % 2][:, :, :], w1n,
                                  DR[w1n][l, :, c0:c0 + 256].rearrange("(k p) c -> p k c", p=128))
                            wload(w3b[gi % 2], w3b[gi % 2][:, :, :], w3n,
                                  DR[w3n][l, :, c0:c0 + 256].rearrange("(k p) c -> p k c", p=128))

                        def ld2(oc):
                            wload(w2b[oc % 2], w2b[oc % 2][:, :, :], w2n,
                                  DR[w2n][l, :, oc * 128:(oc + 1) * 128].rearrange("(k p) c -> p k c", p=128))

                        ld13(0)
                        cnt = 0
                        for gi in range(11):
                            if gi + 1 < 11:
                                ld13(gi + 1)
                            else:
                                ld2(0)
                            for hh in range(2):
                                hc = gi * 2 + hh
                                for tt in range(2):
                                    p1 = PS[cnt % 2]
                                    p3 = PS[2 + cnt % 2]
                                    sl = sil[cnt % 2]
                                    cnt += 1
                                    for k in range(8):
                                        mm(p1[:, :], w1b[gi % 2][:, k, hh * 128:(hh + 1) * 128],
                                           u[:, k, tt * 512:(tt + 1) * 512], k == 0, k == 7, [w1b[gi % 2], u], [p1])
                                    for k in range(8):
                                        mm(p3[:, :], w3b[gi % 2][:, k, hh * 128:(hh + 1) * 128],
                                           u[:, k, tt * 512:(tt + 1) * 512], k == 0, k == 7, [w3b[gi % 2], u], [p3])
                                    fw.op("act", lambda e: e.activation(out=sl[:, :], in_=p1[:, :], func=AF.Silu),
                                          [p1], [sl])
                                    fw.op("dve", lambda e: e.tensor_tensor(
                                        out=gT[:, hc, tt * 512:(tt + 1) * 512], in0=sl[:, :], in1=p3[:, :],
                                        op=ALU.mult), [sl, p3], [gT])
                        for oc in range(8):
                            if oc + 1 < 8:
                                ld2(oc + 1)
                            for tt in range(2):
                                po = PS[4 + (oc * 2 + tt) % 2]
                                for hc in range(22):
                                    mm(po[:, :], w2b[oc % 2][:, hc, :], gT[:, hc, tt * 512:(tt + 1) * 512],
                                       hc == 0, hc == 21, [w2b[oc % 2], gT], [po])
                                hs = hT[:, oc, t0 + tt * 512:t0 + (tt + 1) * 512]
                                fw.op("dve", lambda e: e.scalar_tensor_tensor(
                                    out=hs, in0=po[:, :], scalar=0.5, in1=hs, op0=ALU.mult, op1=ALU.add),
                                    [po, hT], [hT])

            def step_ple(l):
                gcol = l * GL + 29
                with fw.scope() as S:
                    u = S.sbuf("ple_u", (128, 8, 512), BF16)
                    wg = S.sbuf("ple_wg", (128, 8, D), BF16)
                    wp = S.sbuf("ple_wp", (128, 2, D), BF16)
                    pt = [S.sbuf("ple_p%d" % i, (128, 2, 512), BF16) for i in range(2)]
                    sg = [S.sbuf("ple_sg%d" % i, (128, 512), F32) for i in range(2)]
                    for kk in range(2):
                        wload(wg, wg[:, kk * 4:(kk + 1) * 4, :], "w_ple_gate",
                              DR["w_ple_gate"][l, kk * 512:(kk + 1) * 512, :].rearrange("(k p) c -> p k c", p=128))
                    wload(wp, wp[:, :, :], "w_ple_proj",
                          DR["w_ple_proj"][l, :, :].rearrange("(k p) c -> p k c", p=128))
                    for tt in range(4):
                        ts = slice(tt * 512, (tt + 1) * 512)
                        ptt = pt[tt % 2]
                        wload(ptt, ptt[:, :, :], "pT", DR["pT"][l, :, ts].rearrange("(k p) c -> p k c", p=128))
                        norm(lambda k: hT[:, k, ts], [hT], 8, 512, D, gcol, lambda k: u[:, k, :], u)
                        for oc in range(8):
                            pg = PS[oc % 2]
                            pp = PS[2 + oc % 2]
                            s = sg[oc % 2]
                            for k in range(8):
                                mm(pg[:, :], wg[:, k, oc * 128:(oc + 1) * 128], u[:, k, :], k == 0, k == 7, [wg, u], [pg])
                            for k in range(2):
                                mm(pp[:, :], wp[:, k, oc * 128:(oc + 1) * 128], ptt[:, k, :], k == 0, k == 1, [wp, ptt], [pp])
                            fw.op("act", lambda e: e.activation(out=s[:, :], in_=pg[:, :], func=AF.Sigmoid), [pg], [s])
                            fw.op("dve", lambda e: e.tensor_tensor(out=s[:, :], in0=s[:, :], in1=pp[:, :], op=ALU.mult),
                                  [s, pp], [s])
                            fw.op("dve", lambda e: e.tensor_tensor(out=hT[:, oc, ts], in0=hT[:, oc, ts], in1=s[:, :],
                                                                   op=ALU.add), [s, hT], [hT])

            def step_final():
                with fw.scope() as S:
                    o = [S.sbuf("fin_o%d" % i, (128, 8, 512), F32) for i in range(2)]
                    for tt in range(4):
                        ts = slice(tt * 512, (tt + 1) * 512)
                        ot = o[tt % 2]
                        norm(lambda k: hT[:, k, ts], [hT], 8, 512, D, 2 * GL, lambda k: ot[:, k, :], ot)
                        fw.dma("sp", TD["outT"], ot, DR["outT"][:, ts].rearrange("(k p) c -> p k c", p=128), ot[:, :, :], store=True)

            def step_kv(l):
                gb = l * GL
                with fw.scope() as S:
                    u = S.sbuf("kv_u", (128, 8, 512), BF16)
                    wkv = S.sbuf("kv_w", (128, 8, 2368), BF16)
                    wukv = S.sbuf("kv_wukv", (128, 2, 1024), BF16)
                    ckv32 = S.sbuf("kv_ckv32", (128, 2, 512), F32)
                    ckvn = S.sbuf("kv_ckvn", (128, 2, 512), BF16)
                    ropet = S.sbuf("kv_rope", (96, 2, 512), F32)
                    rt = [S.sbuf("kv_rt%d" % i, (96, 512), F32) for i in range(2)]
                    wqa = S.sbuf("kv_wq", (128, 8, 1408), BF16)
                    wuqp_t = S.sbuf("kv_wuqp", (128, 3, 768), BF16)
                    wuqs_t = S.sbuf("kv_wuqs", (128, 3, 768), BF16)
                    cq32 = S.sbuf("kv_cq32", (128, 3, 512), F32)
                    cqn = S.sbuf("kv_cqn", (128, 3, 512), BF16)
                    stg = [S.sbuf("kv_stg%d" % i, (128, 512), BF16) for i in range(4)]
                    sc = [0]

                    def stage():
                        sc[0] += 1
                        return stg[sc[0] % 4]

                    segs = [(0, 384, 256, "w_in"), (256, 640, 32, "w_in"), (288, 0, 32, "w_kr_s"),
                            (320, C_DIFF + 512, 512, "w_in"), (832, C_DIFF + 1024, 512, "w_in"),
                            (1344, C_SB + 512, 512, "w_in"), (1856, C_SB + 1024, 512, "w_in")]
                    for (d0, s0, n, wn) in segs:
                        wload(wkv, wkv[:, :, d0:d0 + n], wn,
                              DR[wn][l, :, s0:s0 + n].rearrange("(k p) c -> p k c", p=128))
                    wload(wukv, wukv[:, :, :], "w_ukv_p", DR["w_ukv_p"][l, :, :].rearrange("(k p) c -> p k c", p=128))
                    for (d0, s0, n) in [(0, 0, 384), (384, C_DIFF, 512), (896, C_SB, 512)]:
                        wload(wqa, wqa[:, :, d0:d0 + n], "w_in", DR["w_in"][l, :, s0:s0 + n].rearrange("(k p) c -> p k c", p=128))
                    wload(wuqp_t, wuqp_t[:, :, :], "w_uq_p", DR["w_uq_p"][l, :, :].rearrange("(k p) c -> p k c", p=128))
                    wload(wuqs_t, wuqs_t[:, :, :], "w_uq_s", DR["w_uq_s"][l, :, :].rearrange("(k p) c -> p k c", p=128))
                    Vm, Vd, Vs = DR["Vm_x"], DR["Vd_x"], DR["Vs_x"]
                    pc = [0]

                    def nps():
                        pc[0] += 1
                        return PS[pc[0] % 6]

                    for tt in range(4):
                        ts = slice(tt * 512, (tt + 1) * 512)
                        norm(lambda k: hT[:, k, ts], [hT], 8, 512, D, gb + 8, lambda k: u[:, k, :], u)
                        fw.dma("sp", ropet, TD["rope"], ropet[0:32, :, :], DR["rope"][:, :, ts])
                        fw.dma("sp", ropet, TD["rope"], ropet[64:96, :, :], DR["rope"][:, :, ts])
                        for ch in range(2):
                            p = nps()
                            for k in range(8):
                                mm(p[:, :], wkv[:, k, ch * 128:(ch + 1) * 128], u[:, k, :], k == 0, k == 7, [wkv, u], [p])
                            evac(ckv32[:, ch, :], p[:, :], [p], [ckv32])
                        norm(lambda k: ckv32[:, k, :], [ckv32], 2, 512, 256, gb + 19, lambda k: ckvn[:, k, :], ckvn)
                        pa = nps()
                        pb = nps()
                        for k in range(8):
                            mm(pa[0:32, :], wkv[:, k, 256:288], u[:, k, :], k == 0, k == 7, [wkv, u], [pa])
                        for k in range(8):
                            mm(pb[0:32, :], wkv[:, k, 288:320], u[:, k, :], k == 0, k == 7, [wkv, u], [pb])
                        fw.op("dve", lambda e: e.tensor_tensor(out=rt[0][0:32, :], in0=pa[0:32, :], in1=ropet[0:32, 0, :],
                                                               op=ALU.mult), [pa, ropet], [rt[0]])
                        fw.op("dve", lambda e: e.tensor_tensor(out=rt[1][0:32, :], in0=pb[0:32, :], in1=ropet[0:32, 1, :],
                                                               op=ALU.mult), [pb, ropet], [rt[1]])
                        sg_ = stage()
                        fw.op("dve", lambda e: e.tensor_tensor(out=sg_[0:32, :], in0=rt[0][0:32, :], in1=rt[1][0:32, :],
                                                               op=ALU.add), [rt[0], rt[1]], [sg_])
                        fw.dma("sp", TD["KTr_x"], sg_, DR["KTr_x"][0:32, ts], sg_[0:32, :], store=True)
                        for grp in range(4):
                            p = nps()
                            for k in range(2):
                                mm(p[:, :], wukv[:, k, grp * 128:(grp + 1) * 128], ckvn[:, k, :], k == 0, k == 1,
                                   [wukv, ckvn], [p])
                            sg_ = stage()
                            evac(sg_[:, :], p[:, :], [p], [sg_])
                            fw.dma("sp", TD["KTm_x"], sg_, DR["KTm_x"][grp * 128:(grp + 1) * 128, ts], sg_[:, :], store=True)
                        for (c0, kn) in [(320, "KTd_x"), (1344, "KTs_x")]:
                            for grp in range(4):
                                p = nps()
                                for k in range(8):
                                    mm(p[:, :], wkv[:, k, c0 + grp * 128:c0 + (grp + 1) * 128], u[:, k, :],
                                       k == 0, k == 7, [wkv, u], [p])
                                sg_ = stage()
                                evac(sg_[:, :], p[:, :], [p], [sg_])
                                fw.dma("sp", TD[kn], sg_, DR[kn][grp * 128:(grp + 1) * 128, ts], sg_[:, :], store=True)
                        for ch in range(3):
                            p = nps()
                            for k in range(8):
                                mm(p[:, :], wqa[:, k, ch * 128:(ch + 1) * 128], u[:, k, :], k == 0, k == 7, [wqa, u], [p])
                            evac(cq32[:, ch, :], p[:, :], [p], [cq32])
                        norm(lambda k: cq32[:, k, :], [cq32], 3, 512, 384, gb + 16, lambda k: cqn[:, k, :], cqn)
                        for h in range(8):
                            pa = nps()
                            pb = nps()
                            for k in range(3):
                                mm(pa[0:96, :], wuqp_t[:, k, h * 96:(h + 1) * 96], cqn[:, k, :], k == 0, k == 2, [wuqp_t, cqn], [pa])
                            for k in range(3):
                                mm(pb[0:96, :], wuqs_t[:, k, h * 96:(h + 1) * 96], cqn[:, k, :], k == 0, k == 2, [wuqs_t, cqn], [pb])
                            fw.op("dve", lambda e: e.tensor_tensor(out=rt[0][64:96, :], in0=pa[64:96, :], in1=ropet[64:96, 0, :], op=ALU.mult),
                                  [pa, ropet], [rt[0]])
                            fw.op("dve", lambda e: e.tensor_tensor(out=rt[1][64:96, :], in0=pb[64:96, :], in1=ropet[64:96, 1, :], op=ALU.mult),
                                  [pb, ropet], [rt[1]])
                            sg_ = stage()
                            fw.op("dve", lambda e: e.tensor_tensor(out=sg_[64:96, :], in0=rt[0][64:96, :], in1=rt[1][64:96, :], op=ALU.add),
                                  [rt[0], rt[1]], [sg_])
                            fw.op("act", lambda e: e.activation(out=sg_[0:64, :], in_=pa[0:64, :], func=AF.Copy), [pa], [sg_])
                            fw.dma("sp", TD["Qm_s"], sg_, DR["Qm_s"][h * 96:(h + 1) * 96, ts], sg_[0:96, :], store=True)
                        for (c0, qn) in [(384, "Qd_s"), (896, "Qs_s")]:
                            for grp in range(4):
                                p = nps()
                                for k in range(8):
                                    mm(p[:, :], wqa[:, k, c0 + grp * 128:c0 + (grp + 1) * 128], u[:, k, :],
                                       k == 0, k == 7, [wqa, u], [p])
                                sg_ = stage()
                                evac(sg_[:, :], p[:, :], [p], [sg_])
                                fw.dma("sp", TD[qn], sg_, DR[qn][grp * 128:(grp + 1) * 128, ts], sg_[:, :], store=True)
                        for tb in range(4):
                            i = tt * 4 + tb
                            tsl = slice(tb * 128, (tb + 1) * 128)
                            p = nps()
                            for k in range(2):
                                mm(p[:, :], ckvn[:, k, tsl], wukv[:, k, 512:1024], k == 0, k == 1, [wukv, ckvn], [p])
                            sg_ = stage()
                            evac(sg_[:, :], p[:, :], [p], [sg_])
                            fw.dma("sp", TD["Vm_x"], sg_,
                                   Vm.ap().rearrange("(h p) (i d) -> p h i d", p=128, d=64)[:, :, i, :],
                                   sg_[:, :].rearrange("p (h d) -> p h d", d=64), store=True)
                            p = nps()
                            for k in range(8):
                                mm(p[:, :], u[:, k, tsl], wkv[:, k, 832:1344], k == 0, k == 7, [wkv, u], [p])
                            sg_ = stage()
                            evac(sg_[:, :], p[:, :], [p], [sg_])
                            fw.dma("sp", TD["Vd_x"], sg_,
                                   Vd.ap().rearrange("(h p) (i d) -> p h i d", p=128, d=128)[:, :, i, :],
                                   sg_[:, :].rearrange("p (h d) -> p h d", d=128), store=True)
                            p = nps()
                            for k in range(8):
                                mm(p[:, :], u[:, k, tsl], wkv[:, k, 1856:2368], k == 0, k == 7, [wkv, u], [p])
                            sg_ = stage()
                            evac(sg_[:, :], p[:, :], [p], [sg_])
                            fw.dma("sp", TD["Vs_x"], sg_,
                                   Vs.ap().rearrange("(h p) (i d) -> p h i d", p=128, d=64)[:, :, i, :],
                                   sg_[:, :].rearrange("p (h d) -> p h d", d=64), store=True)

            def step_xchg():
                for k in XS:
                    src, dst = DR[k + "_x"], DR[k + "_g"]
                    fw.collective(TD[k + "_g"], TD[k + "_x"], lambda e: e.collective_compute(
                        "AllGather", ALU.bypass, replica_groups=[[0, 1], [2, 3], [4, 5], [6, 7]],
                        ins=[src.ap().opt()], outs=[dst.ap().opt()]))

            def step_attn(l, parts=None):
                gb = l * GL
                lam_init = 0.8 - 0.6 * math.exp(-0.3 * l)
                with fw.scope() as S:
                    u = S.sbuf("at_u", (128, 8, 512), BF16)
                    ymla = S.sbuf("at_ymla", (64, 8, 512), BF16)
                    ysb = S.sbuf("at_ysb", (64, 8, 512), BF16)
                    ydf = S.sbuf("at_ydf", (128, 4, 512), BF16)
                    lams = S.sbuf("at_lams", (128, 4), F32)
                    neglam = S.sbuf("at_neglam", (128, 1), F32)
                    tri = S.sbuf("at_tri", (128, 128), BF16)
                    negones = S.sbuf("at_negones", (128, 128), BF16)
                    ident = S.sbuf("at_ident", (128, 128), BF16)
                    zeros_b = S.sbuf("at_zerosb", (128, 128), BF16)
                    ones_b = S.sbuf("at_onesb", (128, 128), BF16)
                    mask_i = S.sbuf("at_maski", (128, 8, 512), BF16)
                    mask_s = S.sbuf("at_masks", (128, 8, 512), BF16)
                    Qt = [S.sbuf("at_q%d" % i, (128, 512), BF16) for i in range(2)]
                    Kt = [S.sbuf("at_k%d" % i, (128, 2, 16, 128), BF16) for i in range(2)]
                    Vt = [S.sbuf("at_v%d" % i, (128, 2, 16, 128), BF16) for i in range(2)]
                    f32s = [S.sbuf("at_f%d" % i, (128, 512), F32) for i in range(5)]
                    b16s = [S.sbuf("at_b%d" % i, (128, 512), BF16) for i in range(9)]
                    wo = [S.sbuf("ot_wo%d" % i, (128, 20, 128), BF16) for i in range(2)]
                    wg = [S.sbuf("ot_wg%d" % i, (128, 3, 8, 128), BF16) for i in range(2)]
                    wout = S.sbuf("ot_wout", (128, 8, D), BF16)

                    lamt, lamp = f32s[0], f32s[1]
                    fw.dma("sp", lamt, TD["lamv"], lamt[:, 0:256], DR["lamv"][:, l * 256:(l + 1) * 256])
                    for j in range(2):
                        fw.op("dve", lambda e: e.tensor_tensor(out=lamp[:, j * 64:(j + 1) * 64], in0=lamt[:, (2 * j) * 64:(2 * j + 1) * 64],
                                                               in1=lamt[:, (2 * j + 1) * 64:(2 * j + 2) * 64], op=ALU.mult),
                              [lamt], [lamp])
                        fw.op("dve", lambda e: e.reduce_sum(out=lams[:, j:j + 1], in_=lamp[:, j * 64:(j + 1) * 64], axis=AX.X),
                              [lamp], [lams])
                    fw.op("act", lambda e: e.activation(out=lams[:, 2:4], in_=lams[:, 0:2], func=AF.Exp), [lams], [lams])
                    fw.op("dve", lambda e: e.scalar_tensor_tensor(out=neglam[:, :], in0=lams[:, 3:4], scalar=-lam_init,
                                                                  in1=lams[:, 2:3], op0=ALU.add, op1=ALU.subtract),
                          [lams], [neglam])
                    fw.op("pool", lambda e: e.memset(zeros_b[:, :], 0.0), [], [zeros_b])
                    fw.op("pool", lambda e: e.memset(ones_b[:, :], 1.0), [], [ones_b])
                    fw.op("pool", lambda e: e.memset(tri[:, :], 1.0), [], [tri])
                    fw.op("pool", lambda e: e.affine_select(out=ident[:, :], in_=tri[:, :], pattern=[[-1, 128]],
                                                            compare_op=ALU.is_equal, fill=0.0, base=0, channel_multiplier=1),
                          [tri], [ident])
                    fw.op("pool", lambda e: e.memset(negones[:, :], -8.0), [], [negones])
                    fw.op("pool", lambda e: e.affine_select(out=tri[:, :], in_=negones[:, :], pattern=[[-1, 128]],
                                                            compare_op=ALU.is_ge, fill=0.0, base=0, channel_multiplier=1),
                          [negones, ident], [tri])
                    fw.dma("sp", mask_i, TD["negm_incl"], mask_i[:, :, :], DR["negm_incl"][:, :, :])
                    fw.dma("sp", mask_s, TD["negm_strict"], mask_s[:, :, :], DR["negm_strict"][:, :, :])
                    for t_ in Kt + Vt:
                        fw.op("pool", lambda e: e.memset(t_[:, :, :, :], 0.0), [], [t_])
                    for t_ in Qt:
                        fw.op("dve", lambda e: e.memset(t_[:, :], 0.0), [], [t_])
                    for kk in range(2):
                        wload(wout, wout[:, kk * 4:(kk + 1) * 4, :], "w_out",
                              DR["w_out"][l, kk * 512:(kk + 1) * 512, :].rearrange("(k p) c -> p k c", p=128))

                    def ld_o(oc):
                        cs = slice(oc * 128, (oc + 1) * 128)
                        b = wo[oc % 2]
                        wload(b, b[0:64, 0:8, :], "w_o_mla", DR["w_o_mla"][l, :, cs].rearrange("(h p) c -> p h c", p=64))
                        wload(b, b[0:64, 8:16, :], "w_o_sb", DR["w_o_sb"][l, :, cs].rearrange("(h p) c -> p h c", p=64))
                        wload(b, b[:, 16:20, :], "w_o_diff", DR["w_o_diff"][l, :, cs].rearrange("(h p) c -> p h c", p=128))
                        for x in range(3):
                            wload(wg[oc % 2], wg[oc % 2][:, x, :, :], "w_branch_gate",
                                  DR["w_branch_gate"][l, :, x * D + oc * 128:x * D + (oc + 1) * 128]
                                  .rearrange("(k p) c -> p k c", p=128))

                    units = [("mla", h) for h in range(8)] + [("diff", h) for h in range(8)] + [("sb", h) for h in range(8)]

                    for g in range(4):
                        gs = slice(g * 512, (g + 1) * 512)
                        nkb = 8 * g + 8
                        nh = nkb // 2
                        norm(lambda k: hT[:, k, gs], [hT], 8, 512, D, gb + 8, lambda k: u[:, k, :], u)

                        def load_k(kt, rows, kn, r0, d0=0):
                            R_ = XS[kn][0]
                            for r in range(2):
                                fw.dma("sp", kt, TD[kn + "_g"], kt[d0:d0 + rows, r, 0:nh, :],
                                       DR[kn + "_g"][r * R_ + r0:r * R_ + r0 + rows, 0:nh * 128]
                                       .rearrange("d (i k) -> d i k", k=128))

                        def load_v(vt, name, h, dv):
                            src = DR[name].ap().rearrange("(r h p) (i d) -> r h p i d", r=2, p=128, d=dv)
                            for r in range(2):
                                fw.dma("sp", vt, TD[name], vt[:, r, 0:nh, 0:dv], src[r, h, :, 0:nh, :])

                        def load_unit(idx):
                            kind, h = units[idx]
                            kt, qt = Kt[idx % 2], Qt[idx % 2]
                            if kind == "mla":
                                load_k(kt, 64, "KTm", h * 64, 0)
                                load_k(kt, 32, "KTr", 0, 64)
                                load_v(Vt[h % 2], "Vm_g", h, 64)
                                fw.dma("sp", qt, TD["Qm_s"], qt[0:96, :], DR["Qm_s"][h * 96:(h + 1) * 96, gs])
                            elif kind == "diff":
                                load_k(kt, 64, "KTd", h * 64, 0)
                                fw.dma("sp", kt, TD["kb"], kt[64:68, :, :, :],
                                       DR["kb"][h // 2, :, :].rearrange("d (r i k) -> d r i k", r=2, k=128))
                                if h % 2 == 0:
                                    load_v(Vt[(h // 2) % 2], "Vd_g", h // 2, 128)
                                if h < 2:
                                    fw.op("dve", lambda e: e.memset(qt[64:128, :], 0.0), [], [qt])
                                fw.dma("sp", qt, TD["Qd_s"], qt[0:64, :], DR["Qd_s"][h * 64:(h + 1) * 64, gs])
                                fw.dma("sp", qt, TD["qb"], qt[64:68, :], DR["qb"][h // 2, :, gs])
                            else:
                                load_k(kt, 64, "KTs", h * 64, 0)
                                load_v(Vt[h % 2], "Vs_g", h, 64)
                                if h < 2:
                                    fw.op("dve", lambda e: e.memset(qt[64:128, :], 0.0), [], [qt])
                                fw.dma("sp", qt, TD["Qs_s"], qt[0:64, :], DR["Qs_s"][h * 64:(h + 1) * 64, gs])

                        def kblk(kt, j):
                            return kt[:, j % 2, j // 2, :]

                        def vblk(vt, j):
                            return vt[:, j % 2, j // 2, :]

                        def q0_of(j):
                            return ((j - 8 * g) // 2) * 128 if j >= 8 * g else 0

                        def softmax_loop(kt, vt, q_t, scale, po, pd):
                            LA = 2

                            def qk(j):
                                q0 = q0_of(j)
                                dg = j >= 8 * g
                                mm(PS[j % 4][:, q0:], kblk(kt, j), q_t[:, q0:], True, not dg, [kt, q_t], [PS[j % 4]])
                                if dg:
                                    mm(PS[j % 4][:, q0:], ident[:, :], mask_i[:, j - 8 * g, q0:], False, True, [ident, mask_i], [PS[j % 4]])

                            for j in range(min(LA, nkb)):
                                qk(j)
                            for j in range(nkb):
                                pS = PS[j % 4]
                                P = b16s[j % 4]
                                q0 = q0_of(j)
                                fw.op("act", lambda e: e.activation(out=P[:, q0:], in_=pS[:, q0:], func=AF.Exp, scale=scale), [pS], [P])
                                if j + LA < nkb:
                                    qk(j + LA)
                                mm(po[:, q0:], vblk(vt, j), P[:, q0:], j == 0, j == nkb - 1, [vt, P], [po])
                                mm(pd[:, q0:], ones_b[:, :], P[:, q0:], j == 0, j == nkb - 1, [ones_b, P], [pd])

                        def recip_act(dst, src_ps, npart):
                            fw.op("act", lambda e: e.activation(out=dst[0:npart, :], in_=src_ps[0:npart, :], func=AF.Ln), [src_ps], [dst])
                            fw.op("act", lambda e: e.activation(out=dst[0:npart, :], in_=dst[0:npart, :], func=AF.Exp, scale=-1.0), [dst], [dst])

                        def unit_mla(idx, h):
                            kt, vt, qt = Kt[idx % 2], Vt[h % 2], Qt[idx % 2]
                            po, pd = PS[4 + h % 2], PS[6 + h % 2]
                            softmax_loop(kt, vt, qt, 96 ** -0.5, po, pd)
                            rc = f32s[2 + h % 2]
                            recip_act(rc, pd, 64)
                            fw.op("dve", lambda e: e.tensor_tensor(out=ymla[:, h, :], in0=po[0:64, :], in1=rc[0:64, :], op=ALU.mult),
                                  [po, rc], [ymla])

                        def unit_diff(idx, hc):
                            h, c = hc // 2, hc % 2
                            kt, vt, qt = Kt[idx % 2], Vt[h % 2], Qt[idx % 2]
                            po, pdn = PS[4 + c], PS[6 + c]
                            softmax_loop(kt, vt, qt, 0.125, po, pdn)
                            r0, r1, a0, a1, od = f32s[0], f32s[1], f32s[2], f32s[3], f32s[4]
                            if c == 0:
                                recip_act(r0, PS[6], 128)
                                fw.op("dve", lambda e: e.tensor_tensor(out=a0[:, :], in0=PS[4][:, :], in1=r0[:, :], op=ALU.mult), [PS[4], r0], [a0])
                            if c == 1:
                                recip_act(r1, PS[7], 128)
                                fw.op("dve", lambda e: e.tensor_tensor(out=a1[:, :], in0=PS[5][:, :], in1=r1[:, :], op=ALU.mult), [PS[5], r1], [a1])
                                fw.op("dve", lambda e: e.scalar_tensor_tensor(out=od[:, :], in0=a1[:, :], scalar=neglam[:, 0:1], in1=a0[:, :],
                                                                              op0=ALU.mult, op1=ALU.add), [a1, a0, neglam], [od])
                                norm(lambda k: od[:, :], [od], 1, 512, 128, gb + 37, lambda k: ydf[:, h, :], ydf,
                                     bias2=math.log(1.0 - lam_init))

                        def unit_sb(idx, h):
                            kt, vt, qh = Kt[idx % 2], Vt[h % 2], Qt[idx % 2]
                            po = PS[6 + h % 2]
                            acc32 = f32s[3]
                            ee = [f32s[0], f32s[1], f32s[2]]
                            spb = [b16s[0], b16s[1], b16s[2]]
                            accb = [b16s[3], b16s[4], b16s[5], b16s[6]]
                            wb = [b16s[7], b16s[8]]
                            order = list(range(nkb - 1, -1, -1))
                            fw.op("dve", lambda e: e.memset(acc32[:, :], 0.0), [], [acc32])
                            for ab_ in accb:
                                fw.op("dve", lambda e: e.memset(ab_[:, :], 0.0), [], [ab_])
                            mm(po[:, :], zeros_b[:, :], mask_s[:, 0, :], True, False, [zeros_b, mask_s], [po])

                            def stage1(jj):
                                j = order[jj]
                                q0 = q0_of(j)
                                pA = PS[jj % 3]
                                e_ = ee[jj % 3]
                                sp = spb[jj % 3]
                                dg = j >= 8 * g
                                mm(pA[:, q0:], kblk(kt, j), qh[:, q0:], True, not dg, [kt, qh], [pA])
                                if dg:
                                    mm(pA[:, q0:], ident[:, :], mask_s[:, j - 8 * g, q0:], False, True, [ident, mask_s], [pA])
                                fw.op("act", lambda e: e.activation(out=e_[:, q0:], in_=pA[:, q0:], func=AF.Exp, scale=0.125), [pA], [e_])
                                fw.op("act", lambda e: e.activation(out=sp[:, q0:], in_=e_[:, q0:], func=AF.Ln, bias=1.0), [e_], [sp])
                                if jj + 1 < nkb:
                                    ab = accb[jj % 4]
                                    fw.op("dve", lambda e: e.tensor_tensor(out=ab[:, q0:], in0=acc32[:, q0:], in1=sp[:, q0:], op=ALU.add),
                                          [acc32, sp], [ab])
                                    if jj + 2 < nkb:
                                        fw.op("dve", lambda e: e.tensor_tensor(out=acc32[:, q0:], in0=acc32[:, q0:], in1=sp[:, q0:], op=ALU.add),
                                              [acc32, sp], [acc32])

                            def stage2a(jj):
                                j = order[jj]
                                q0 = q0_of(j)
                                pX = PS[3 + jj % 2]
                                sp = spb[jj % 3]
                                w = wb[jj % 2]
                                mm(pX[:, q0:], kblk(kt, j), qh[:, q0:], True, False, [kt, qh], [pX])
                                if j >= 8 * g:
                                    mm(pX[:, q0:], ident[:, :], mask_s[:, j - 8 * g, q0:], False, False, [ident, mask_s], [pX])
                                mm(pX[:, q0:], tri[:, :], sp[:, q0:], False, jj == 0, [tri, sp], [pX])
                                if jj > 0:
                                    cb = accb[(jj - 1) % 4]
                                    mm(pX[:, q0:], negones[:, :], cb[:, q0:], False, True, [negones, cb], [pX])
                                fw.op("act", lambda e: e.activation(out=w[:, q0:], in_=pX[:, q0:], func=AF.Exp, scale=0.125), [pX], [w])

                            def stage2b(jj):
                                j = order[jj]
                                q0 = q0_of(j)
                                w = wb[jj % 2]
                                mm(po[:, q0:], vblk(vt, j), w[:, q0:], False, jj == nkb - 1, [vt, w], [po])

                            stage1(0)
                            if nkb > 1:
                                stage1(1)
                            for jj in range(nkb):
                                stage2a(jj)
                                if jj + 2 < nkb:
                                    stage1(jj + 2)
                                stage2b(jj)
                            evac(ysb[:, h, :], po[0:64, :], [po], [ysb])

                        load_unit(0)
                        for idx, (kind, h) in enumerate(units):
                            if idx + 1 < len(units):
                                load_unit(idx + 1)
                            if idx == 18:
                                ld_o(0)
                            if kind == "mla":
                                unit_mla(idx, h)
                            elif kind == "diff":
                                unit_diff(idx, h)
                            else:
                                unit_sb(idx, h)

                        sgt = [f32s[0], f32s[1], f32s[2]]
                        acc = f32s[4]
                        for oc in range(8):
                            if oc + 1 < 8:
                                ld_o(oc + 1)
                            b = wo[oc % 2]
                            g_ = wg[oc % 2]
                            pA, pB, pC = PS[0], PS[1], PS[2]
                            for h in range(8):
                                mm(pA[:, :], b[0:64, h, :], ymla[:, h, :], h == 0, h == 7, [b, ymla], [pA])
                            for h in range(4):
                                mm(pB[:, :], b[:, 16 + h, :], ydf[:, h, :], h == 0, h == 3, [b, ydf], [pB])
                            for h in range(8):
                                mm(pC[:, :], b[0:64, 8 + h, :], ysb[:, h, :], h == 0, h == 7, [b, ysb], [pC])
                            for x in range(3):
                                pg = PS[3 + x]
                                for k in range(8):
                                    mm(pg[:, :], g_[:, x, k, :], u[:, k, :], k == 0, k == 7, [g_, u], [pg])
                                fw.op("act", lambda e: e.activation(out=sgt[x][:, :], in_=pg[:, :], func=AF.Sigmoid), [pg], [sgt[x]])
                            fw.op("dve", lambda e: e.tensor_tensor(out=acc[:, :], in0=sgt[0][:, :], in1=pA[:, :], op=ALU.mult),
                                  [sgt[0], pA], [acc])
                            fw.op("dve", lambda e: e.tensor_tensor(out=sgt[1][:, :], in0=sgt[1][:, :], in1=pB[:, :], op=ALU.mult),
                                  [sgt[1], pB], [sgt[1]])
                            fw.op("dve", lambda e: e.tensor_tensor(out=sgt[2][:, :], in0=sgt[2][:, :], in1=pC[:, :], op=ALU.mult),
                                  [sgt[2], pC], [sgt[2]])
                            fw.op("dve", lambda e: e.tensor_tensor(out=acc[:, :], in0=acc[:, :], in1=sgt[1][:, :], op=ALU.add),
                                  [acc, sgt[1]], [acc])
                            fw.op("dve", lambda e: e.tensor_tensor(out=b16s[oc][:, :], in0=acc[:, :], in1=sgt[2][:, :], op=ALU.add),
                                  [acc, sgt[2]], [b16s[oc]])
                        for oc in range(8):
                            p = PS[6 + oc % 2]
                            for k in range(8):
                                mm(p[:, :], wout[:, k, oc * 128:(oc + 1) * 128], b16s[k][:, :], k == 0, k == 7, [wout, b16s[k]], [p])
                            fw.op("dve", lambda e: e.tensor_tensor(out=hT[:, oc, gs], in0=hT[:, oc, gs], in1=p[:, :], op=ALU.add),
                                  [hT, p], [hT])

            for s in steps:
                k = s[0]
                if k == "load_x":
                    step_load("xT")
                elif k == "load_h":
                    step_load("hT_in")
                elif k == "store_h":
                    step_store("hT_out")
                elif k == "ffn":
                    step_ffn(s[1], s[2])
                elif k == "ple":
                    step_ple(s[1])
                elif k == "kv":
                    step_kv(s[1])
                elif k == "xchg":
                    step_xchg()
                elif k == "attn":
                    step_attn(s[1], *s[2:])
                elif k == "final":
                    step_final()
                elif k == "dbg":
                    pass
                else:
                    raise ValueError(k)
            fw.barrier()
        print("bass program: n_ins=%d n_wait=%d" % (fw.n_ins, fw.n_wait))
    nc.declared_inputs = set(DR.keys())
    return nc


def own_pos(c):
    return ((2 * np.arange(16)[:, None] + c) * 128 + np.arange(128)[None, :]).reshape(-1)


def prep_common(inp):
    f = lambda a: np.ascontiguousarray(np.asarray(a, dtype=np.float32))
    W = {}
    for k in ["w1_a", "w3_a", "w2_a", "w1_b", "w3_b", "w2_b", "w_in", "w_o_mla", "w_o_diff", "w_o_sb",
              "w_branch_gate", "w_out", "w_ple_gate", "w_ple_proj"]:
        W[k] = f(inp[k])
    w_in = W["w_in"]
    kr = w_in[:, :, 640:672]
    W["w_kr_s"] = f(np.concatenate([kr[:, :, 16:32], kr[:, :, 0:16]], axis=2))
    wuq = f(inp["w_uq"]).reshape(NL, 384, 8, 96)
    nope, ropec = wuq[..., :64], wuq[..., 64:]
    W["w_uq_p"] = f(wuq.reshape(NL, 384, 768))
    W["w_uq_s"] = f(np.concatenate([np.zeros_like(nope), ropec[..., 16:], ropec[..., :16]], axis=3).reshape(NL, 384, 768))
    wukv = f(inp["w_ukv"]).reshape(NL, 256, 8, 128)
    W["w_ukv_p"] = f(np.concatenate([wukv[..., :64].reshape(NL, 256, 512), wukv[..., 64:].reshape(NL, 256, 512)], axis=2))
    gv = np.zeros((128, NGV), np.float32)

    def put(col, vec):
        v = np.asarray(vec, np.float32).reshape(-1, 128)
        gv[:, col:col + v.shape[0]] = v.T

    for l in range(NL):
        b = l * GL
        put(b + 0, inp["g_ffn1"][l]); put(b + 8, inp["g_mix"][l]); put(b + 16, inp["g_cq"][l])
        put(b + 19, inp["g_ckv"][l]); put(b + 21, inp["g_ffn2"][l]); put(b + 29, inp["g_ple"][l])
        put(b + 37, inp["g_subln"][l])
    put(2 * GL, inp["g_final"])
    W["gvec"] = gv
    lam = np.stack([np.asarray(inp[k], np.float32) for k in ["lambda_q1", "lambda_k1", "lambda_q2", "lambda_k2"]], axis=1)
    W["lamv"] = f(np.broadcast_to(lam.reshape(1, NL * 4 * 64), (128, NL * 4 * 64)))
    return W


def prep_core(inp, core, need):
    b, c = core // 2, core % 2
    pos = own_pos(c)
    bf = ml_dtypes.bfloat16
    m = {}
    if "xT" in need:
        m["xT"] = np.ascontiguousarray(np.asarray(inp["x"], np.float32)[b][pos].T)
    m["pT"] = np.ascontiguousarray(np.asarray(inp["p"], np.float32)[:, b][:, pos].transpose(0, 2, 1))
    half = 16
    freqs = (1.0 / (10000.0 ** (np.arange(half, dtype=np.float32) / half))).astype(np.float32)
    ang = pos.astype(np.float32)[:, None] * freqs[None, :]
    cs, sn = np.cos(ang).astype(np.float32), np.sin(ang).astype(np.float32)
    rope = np.zeros((32, 2, NT), np.float32)
    rope[:16, 0] = cs.T; rope[16:, 0] = cs.T
    rope[:16, 1] = -sn.T; rope[16:, 1] = sn.T
    m["rope"] = rope
    kp = np.arange(128)[:, None, None]
    jj = np.arange(8)[None, :, None]
    q = np.arange(512)[None, None, :]
    kpos = jj * 128 + kp
    qpos = (c + 2 * (q // 128)) * 128 + q % 128
    m["negm_incl"] = np.where(kpos <= qpos, 0.0, -30000.0).astype(np.float32).astype(bf)
    m["negm_strict"] = np.where(kpos < qpos, 0.0, -30000.0).astype(np.float32).astype(bf)
    qb = np.zeros((4, 4, NT), np.float32)
    kb = np.zeros((4, 4, 2, 16, 128), np.float32)
    gk = ((2 * np.arange(16)[None, :, None] + np.arange(2)[:, None, None]) * 128 + np.arange(128)[None, None, :])
    for h in range(4):
        s8 = 8.0 * 2.0 ** (-2.0 * (h + 1))
        qb[h, 0] = 1.0; qb[h, 1] = 1.0
        qb[h, 2] = -(pos // 64 * 64) * s8; qb[h, 3] = -(pos % 64) * s8
        kb[h, 0] = (gk // 64 * 64) * s8; kb[h, 1] = (gk % 64) * s8
        kb[h, 2] = 1.0; kb[h, 3] = 1.0
    m["qb"] = qb.astype(bf)
    m["kb"] = kb.reshape(4, 4, 4096).astype(bf)
    return m


LAYER = lambda l: [("ffn", l, "a"), ("kv", l), ("xchg",), ("attn", l), ("ffn", l, "b"), ("ple", l)]
PLAN_FUSED = ["load_x"] + LAYER(0) + LAYER(1) + ["final"]


def run(nc, maps):
    maps = [{k: v for k, v in m.items() if k in nc.declared_inputs} for m in maps]
    res = run_bass_kernel_spmd(nc, maps, core_ids=list(range(8)))
    return res.results


def kernel(**inputs):
    W = prep_common(inputs)
    cores = [prep_core(inputs, c, ["xT"]) for c in range(8)]
    nc = build(PLAN_FUSED, True)
    res = run(nc, [dict(W, **cores[c]) for c in range(8)])
    out = np.zeros((4, 4096, D), np.float32)
    for c in range(8):
        out[c // 2][own_pos(c % 2)] = res[c]["outT"].T
    return out
```
